# Optimizing a Trainium2 kernel written in Bass

```python
import functools
import jax, jax.numpy as jnp
from jax import lax
import numpy as np

D_MODEL = 1024
BATCH = 1
SEQ = 16384
DEPTH = 1

NSA_HEADS = 8
NSA_KV_GROUPS = 2
NSA_HEAD_DIM = 64
NSA_WIDTH = NSA_HEADS * NSA_HEAD_DIM
NSA_KV_WIDTH = NSA_KV_GROUPS * NSA_HEAD_DIM
CMP_LEN = 32
CMP_STRIDE = 16
CMP_HIDDEN = 256
SEL_BLOCK = 64
SEL_TOPN = 16
WINDOW = 512
Q_BLOCK = 128
ROPE_THETA = 500000.0
ROPE_DIM = NSA_HEAD_DIM // 4

RET_HEADS = 4
RET_QK_DIM = 128
RET_V_DIM = 256
RET_QK_WIDTH = RET_HEADS * RET_QK_DIM
RET_V_WIDTH = RET_HEADS * RET_V_DIM
RET_CHUNK = 128
RET_ROPE_BASE = 10000.0

RMS_EPS = 1e-6
GN_EPS = 1e-6

PROJ_SIZES = (
    NSA_WIDTH,
    NSA_KV_WIDTH,
    NSA_KV_WIDTH,
    NSA_KV_WIDTH,
    NSA_KV_WIDTH,
    NSA_KV_WIDTH,
    NSA_KV_WIDTH,
    NSA_HEADS * 3,
    NSA_WIDTH,
    RET_QK_WIDTH,
    RET_QK_WIDTH,
    RET_V_WIDTH,
    RET_V_WIDTH,
    D_MODEL,
    D_MODEL,
)
PROJ_TOTAL = sum(PROJ_SIZES)

kernel_name = "hybrid_nsa_retention_gated_block"


def rms_norm(x, g):
    x32 = x.astype(jnp.float32)
    y = x32 * lax.rsqrt(jnp.mean(x32 * x32, axis=-1, keepdims=True) + RMS_EPS)
    return (y * g.astype(jnp.float32)).astype(x.dtype)


def split_points():
    pts, acc = [], 0
    for s in PROJ_SIZES[:-1]:
        acc += s
        pts.append(acc)
    return pts


def apply_rotary(x, ang):
    half = ang.shape[-1]
    r = 2 * half
    cos = jnp.cos(ang)[:, None, :].astype(x.dtype)
    sin = jnp.sin(ang)[:, None, :].astype(x.dtype)
    x1 = x[..., :half]
    x2 = x[..., half:r]
    return jnp.concatenate([x1 * cos - x2 * sin, x2 * cos + x1 * sin, x[..., r:]], axis=-1)


def masked_softmax(s, mask):
    s = jnp.where(mask, s.astype(jnp.float32), -jnp.inf)
    m = jnp.max(s, axis=-1, keepdims=True)
    m = jnp.where(jnp.isfinite(m), m, 0.0)
    p = jnp.where(mask, jnp.exp(s - m), 0.0)
    return p / jnp.maximum(jnp.sum(p, axis=-1, keepdims=True), jnp.finfo(jnp.float32).tiny)


def compress_blocks(kv, pe, w1, w2):
    S, G, dh = kv.shape
    n_cmp = (S - CMP_LEN) // CMP_STRIDE + 1
    tok_idx = jnp.arange(n_cmp)[:, None] * CMP_STRIDE + jnp.arange(CMP_LEN)[None, :]
    blocks = kv[tok_idx] + pe[None, :, None, :]
    flat = blocks.transpose(0, 2, 1, 3).reshape(n_cmp, G, CMP_LEN * dh)
    return jax.nn.gelu(flat @ w1) @ w2


def nsa_sequence(q, k_cmp_raw, v_cmp_raw, k_sel, v_sel, k_win, v_win, gate,
                 cmp_pe_k, cmp_w1_k, cmp_w2_k, cmp_pe_v, cmp_w1_v, cmp_w2_v):
    S = q.shape[0]
    G, H, dh = NSA_KV_GROUPS, NSA_HEADS, NSA_HEAD_DIM
    hpg = H // G
    scale = dh ** -0.5

    kc = compress_blocks(k_cmp_raw, cmp_pe_k, cmp_w1_k, cmp_w2_k)
    vc = compress_blocks(v_cmp_raw, cmp_pe_v, cmp_w1_v, cmp_w2_v)
    n_cmp = kc.shape[0]
    cmp_start = jnp.arange(n_cmp) * CMP_STRIDE
    cmp_end = cmp_start + CMP_LEN - 1

    n_sel = S // SEL_BLOCK
    n_top = min(SEL_TOPN, n_sel)
    sel_start = jnp.arange(n_sel) * SEL_BLOCK
    overlap = ((cmp_start[:, None] < sel_start[None, :] + SEL_BLOCK)
               & (cmp_start[:, None] + CMP_LEN > sel_start[None, :])).astype(jnp.float32)
    kb = k_sel.reshape(n_sel, SEL_BLOCK, G, dh).transpose(2, 0, 1, 3)
    vb = v_sel.reshape(n_sel, SEL_BLOCK, G, dh).transpose(2, 0, 1, 3)

    k_pad = jnp.pad(k_win, ((WINDOW, 0), (0, 0), (0, 0)))
    v_pad = jnp.pad(v_win, ((WINDOW, 0), (0, 0), (0, 0)))

    n_qb = S // Q_BLOCK
    qb = q.reshape(n_qb, Q_BLOCK, G, hpg, dh)
    gb = gate.reshape(n_qb, Q_BLOCK, G, hpg, 3)
    blk_ids = jnp.arange(n_sel)
    g_idx = jnp.arange(G)[None, :, None]

    def block(args):
        qi, gi, b = args
        t = b * Q_BLOCK + jnp.arange(Q_BLOCK)
        s = jnp.einsum('qghd,cgd->qghc', qi, kc) * scale
        mask_c = (cmp_end[None, :] <= t[:, None])[:, None, None, :]
        p_c = masked_softmax(s, mask_c)
        o_c = jnp.einsum('qghc,cgd->qghd', p_c.astype(vc.dtype), vc)
        imp = jnp.einsum('qgc,cn->qgn', jnp.sum(p_c, axis=2), overlap)
        cur = t // SEL_BLOCK
        valid = blk_ids[None, :] <= cur[:, None]
        forced = ((blk_ids[None, :] == 0) | (blk_ids[None, :] == cur[:, None])
                  | (blk_ids[None, :] == cur[:, None] - 1))
        score = jnp.where(forced[:, None, :], jnp.inf,
                          jnp.where(valid[:, None, :], imp, -jnp.inf))
        _, idx = lax.top_k(score, n_top)
        sel_valid = jnp.take_along_axis(
            jnp.broadcast_to(valid[:, None, :], score.shape), idx, axis=-1)
        ks = kb[g_idx, idx]
        vs = vb[g_idx, idx]
        s = jnp.einsum('qghd,qgnbd->qghnb', qi, ks) * scale
        kpos = idx[..., None] * SEL_BLOCK + jnp.arange(SEL_BLOCK)
        mask_s = sel_valid[..., None] & (kpos <= t[:, None, None, None])
        nk = n_top * SEL_BLOCK
        p_s = masked_softmax(s.reshape(Q_BLOCK, G, hpg, nk),
                             mask_s.reshape(Q_BLOCK, G, 1, nk))
        o_s = jnp.einsum('qghk,qgkd->qghd', p_s.astype(vs.dtype),
                         vs.reshape(Q_BLOCK, G, nk, dh))
        kw = lax.dynamic_slice_in_dim(k_pad, b * Q_BLOCK, Q_BLOCK + WINDOW, axis=0)
        vw = lax.dynamic_slice_in_dim(v_pad, b * Q_BLOCK, Q_BLOCK + WINDOW, axis=0)
        wpos = b * Q_BLOCK - WINDOW + jnp.arange(Q_BLOCK + WINDOW)
        diff = t[:, None] - wpos[None, :]
        mask_w = ((wpos[None, :] >= 0) & (diff >= 0) & (diff < WINDOW))[:, None, None, :]
        s = jnp.einsum('qghd,kgd->qghk', qi, kw) * scale
        p_w = masked_softmax(s, mask_w)
        o_w = jnp.einsum('qghk,kgd->qghd', p_w.astype(vw.dtype), vw)
        g = jax.nn.sigmoid(gi.astype(jnp.float32))
        o = g[..., 0:1] * o_c + g[..., 1:2] * o_s + g[..., 2:3] * o_w
        return o.reshape(Q_BLOCK, H * dh).astype(q.dtype)

    out = lax.map(block, (qb, gb, jnp.arange(n_qb)))
    return out.reshape(S, H * dh)


def retention_sequence(q, k, v):
    S = q.shape[0]
    H, dk, dv, C = RET_HEADS, RET_QK_DIM, RET_V_DIM, RET_CHUNK
    nc = S // C
    log_g = jnp.log(1.0 - 2.0 ** (-5.0 - jnp.arange(H, dtype=jnp.float32)))
    n = jnp.arange(C, dtype=jnp.float32)
    diff = n[:, None] - n[None, :]
    decay_mask = jnp.where(diff >= 0, jnp.exp(jnp.maximum(diff, 0.0)[None] * log_g[:, None, None]), 0.0)
    q_dec = jnp.exp((n[None, :] + 1.0) * log_g[:, None])
    k_dec = jnp.exp((C - 1.0 - n[None, :]) * log_g[:, None])
    chunk_dec = jnp.exp(C * log_g)

    qc = q.reshape(nc, C, H, dk)
    kc = k.reshape(nc, C, H, dk)
    vc = v.reshape(nc, C, H, dv)
    att = jnp.einsum('nihd,njhd->nhij', qc, kc) * decay_mask[None]
    o_inner = jnp.einsum('nhij,njhe->nihe', att, vc)
    kv = jnp.einsum('njhd,hj,njhe->nhde', kc, k_dec, vc).astype(jnp.float32)

    def step(R, kv_i):
        return chunk_dec[:, None, None] * R + kv_i, R

    _, R_prev = lax.scan(step, jnp.zeros((H, dk, dv), jnp.float32), kv)
    o_cross = jnp.einsum('nihd,hi,nhde->nihe', qc, q_dec, R_prev)
    o = (o_inner + o_cross).reshape(S, H, dv).astype(jnp.float32)
    mu = jnp.mean(o, axis=-1, keepdims=True)
    var = jnp.mean(jnp.square(o - mu), axis=-1, keepdims=True)
    o = (o - mu) * lax.rsqrt(var + GN_EPS)
    return o.reshape(S, H * dv).astype(q.dtype)


def setup_inputs(seed: int = 0) -> dict:
    key = jax.random.key(seed)
    ks = jax.random.split(key, 16)
    f32 = jnp.float32
    nrm = lambda k, shape, fan_in: jax.random.normal(k, shape, f32) * (fan_in ** -0.5)
    cmp_in = CMP_LEN * NSA_HEAD_DIM
    return {
        "x": jax.random.normal(ks[0], (BATCH, SEQ, D_MODEL), f32),
        "norm_pre": 1.0 + 0.05 * jax.random.normal(ks[1], (D_MODEL,), f32),
        "w_in": nrm(ks[2], (D_MODEL, PROJ_TOTAL), D_MODEL),
        "b_nsa_gate": 0.1 * jax.random.normal(ks[3], (NSA_HEADS * 3,), f32),
        "cmp_pe_k": 0.02 * jax.random.normal(ks[4], (CMP_LEN, NSA_HEAD_DIM), f32),
        "cmp_w1_k": nrm(ks[5], (cmp_in, CMP_HIDDEN), cmp_in),
        "cmp_w2_k": nrm(ks[6], (CMP_HIDDEN, NSA_HEAD_DIM), CMP_HIDDEN),
        "cmp_pe_v": 0.02 * jax.random.normal(ks[7], (CMP_LEN, NSA_HEAD_DIM), f32),
        "cmp_w1_v": nrm(ks[8], (cmp_in, CMP_HIDDEN), cmp_in),
        "cmp_w2_v": nrm(ks[9], (CMP_HIDDEN, NSA_HEAD_DIM), CMP_HIDDEN),
        "w_nsa_o": nrm(ks[10], (NSA_WIDTH, D_MODEL), NSA_WIDTH),
        "w_ret_o": nrm(ks[11], (RET_V_WIDTH, D_MODEL), RET_V_WIDTH),
        "w_out": nrm(ks[12], (D_MODEL, D_MODEL), D_MODEL),
        "norm_post": 1.0 + 0.05 * jax.random.normal(ks[13], (D_MODEL,), f32),
    }


def reference(x, norm_pre, w_in, b_nsa_gate, cmp_pe_k, cmp_w1_k, cmp_w2_k,
              cmp_pe_v, cmp_w1_v, cmp_w2_v, w_nsa_o, w_ret_o, w_out, norm_post):
    B, S, _ = x.shape
    pos = jnp.arange(S, dtype=jnp.float32)
    nsa_inv = 1.0 / (ROPE_THETA ** (jnp.arange(0, ROPE_DIM, 2, dtype=jnp.float32) / ROPE_DIM))
    nsa_ang = pos[:, None] * nsa_inv[None, :]
    ret_inv = 1.0 / (RET_ROPE_BASE ** jnp.linspace(0.0, 1.0, RET_QK_DIM // 2, dtype=jnp.float32))
    ret_ang = pos[:, None] * ret_inv[None, :]

    for _ in range(DEPTH):
        h = rms_norm(x, norm_pre)
        proj = h @ w_in
        (q_n, k_c, v_c, k_s, v_s, k_w, v_w, g_n, z_n,
         q_r, k_r, v_r, z_r, m_a, m_b) = jnp.split(proj, split_points(), axis=-1)

        kvshape = (B, S, NSA_KV_GROUPS, NSA_HEAD_DIM)
        q_n = apply_rotary(q_n.reshape(B, S, NSA_HEADS, NSA_HEAD_DIM), nsa_ang)
        k_c = apply_rotary(k_c.reshape(kvshape), nsa_ang)
        k_s = apply_rotary(k_s.reshape(kvshape), nsa_ang)
        k_w = apply_rotary(k_w.reshape(kvshape), nsa_ang)
        gate_n = (g_n + b_nsa_gate).reshape(B, S, NSA_HEADS, 3)
        nsa_fn = lambda a, b_, c, d, e, f, g, gt: nsa_sequence(
            a, b_, c, d, e, f, g, gt,
            cmp_pe_k, cmp_w1_k, cmp_w2_k, cmp_pe_v, cmp_w1_v, cmp_w2_v)
        o_a = jax.vmap(nsa_fn)(q_n, k_c, v_c.reshape(kvshape), k_s, v_s.reshape(kvshape),
                               k_w, v_w.reshape(kvshape), gate_n)
        y_a = (o_a * jax.nn.silu(z_n)) @ w_nsa_o

        q_r = apply_rotary(q_r.reshape(B, S, RET_HEADS, RET_QK_DIM), ret_ang)
        k_r = apply_rotary(k_r.reshape(B, S, RET_HEADS, RET_QK_DIM), ret_ang) * (RET_QK_DIM ** -0.5)
        o_b = jax.vmap(retention_sequence)(q_r, k_r, v_r.reshape(B, S, RET_HEADS, RET_V_DIM))
        y_b = (o_b * jax.nn.silu(z_r)) @ w_ret_o

        merged = jax.nn.sigmoid(m_a) * y_a + jax.nn.sigmoid(m_b) * y_b
        x = x + rms_norm(merged @ w_out, norm_post)
    return x
```

```python
import os
from contextlib import ExitStack
import numpy as np
import ml_dtypes
import concourse.bass as bass
import concourse.mybir as mybir
from concourse.bass_utils import run_bass_kernel_spmd

F32 = mybir.dt.float32
BF16 = mybir.dt.bfloat16
ALU = mybir.AluOpType
AF = mybir.ActivationFunctionType
AX = mybir.AxisListType

NCORES = 8
NR = 16
S = 16384
D = 1024
PT = 6936
NEG = -30000.0
ND = 8

C_QN, C_KC, C_VC, C_KS, C_VS, C_KW, C_VW, C_GN, C_ZN, C_QR, C_KR, C_VR, C_ZR, C_MA, C_MB = (
    0, 512, 640, 768, 896, 1024, 1152, 1280, 1304, 1816, 2328, 2840, 3864, 4888, 5912)


def tile_of(c, r):
    return 8 * r + (c if r % 2 == 0 else 7 - c)


def slot_of(b):
    r = b // 8
    j = b % 8
    c = j if r % 2 == 0 else 7 - j
    return c * NR + r


class Buf:
    __slots__ = ("t", "w", "r", "excl")

    def __init__(self, t, excl=False):
        self.t = t
        self.w = {}
        self.r = {}
        self.excl = excl

    def __getitem__(self, idx):
        return self.t[idx]


class Stream:
    def __init__(self, name, selfsync):
        self.name = name
        self.selfsync = selfsync
        self.count = 0
        self.ndma = 0
        self.seen = {}
        self.ops = []


class Prog:
    def __init__(self, nc, es):
        self.nc = nc
        self.es = es
        self.sems = {}
        self.streams = {}
        for name, ss in (("pe", False), ("act", True), ("dve", True), ("pool", True), ("sp", False)):
            self.streams[name] = Stream(name, ss)
            self.sems[name] = es.enter_context(nc.semaphore("s_" + name))
        for q in ("sp", "pool"):
            for i in range(ND):
                k = "%s_d%d" % (q, i)
                self.sems[k] = es.enter_context(nc.semaphore(k))
        self.ncc = 0

    def sb(self, name, shape, dt):
        return Buf(self.es.enter_context(self.nc.sbuf_tensor("sb_" + name, list(shape), dt)))

    def ps(self, name, shape, dt):
        return Buf(self.es.enter_context(self.nc.psum_tensor(name, list(shape), dt)), excl=True)

    def _waits(self, s, reads, writes, nowaw):
        need = {}
        for b in reads:
            for k, v in b.w.items():
                if need.get(k, 0) < v:
                    need[k] = v
        for b in writes:
            for k, v in b.r.items():
                if need.get(k, 0) < v:
                    need[k] = v
            if not nowaw:
                for k, v in b.w.items():
                    if need.get(k, 0) < v:
                        need[k] = v
        waits = []
        for k, v in need.items():
            if k == s.name and not s.selfsync:
                continue
            if s.seen.get(k, 0) >= v:
                continue
            s.seen[k] = v
            waits.append((k, v))
        return waits

    def _book(self, ev, reads, writes, nowaw):
        k, v = ev
        for b in reads:
            b.r[k] = v
        for b in writes:
            if nowaw:
                b.w[k] = v
            else:
                b.w = {k: v}
                b.r = {}

    def op(self, sname, fn, reads=(), writes=(), nowaw=False):
        s = self.streams[sname]
        if sname != "pe":
            ex = [b for b in reads if b.excl]
            if ex:
                reads = [b for b in reads if not b.excl]
                writes = list(writes) + ex
                nowaw = False
        waits = self._waits(s, reads, writes, nowaw)
        s.count += 1
        s.ops.append((waits, fn, (s.name, 1)))
        self._book((s.name, s.count), reads, writes, nowaw)

    def dma(self, qname, out_ap, in_ap, reads=(), writes=(), nowaw=False, slow=False):
        q = self.streams[qname]
        i = q.ndma
        q.ndma += 1
        k = "%s_d%d" % (qname, i % ND)
        v = 16 * (i // ND + 1)
        waits = self._waits(q, reads, writes, nowaw)
        if v > 16 and q.seen.get(k, 0) < v - 16:
            waits.append((k, v - 16))
            q.seen[k] = v - 16
        if slow:
            fn = lambda e: e.dma_start(out=out_ap, in_=in_ap, allow_slow_non_contiguous=True)
        else:
            fn = lambda e: e.dma_start(out=out_ap, in_=in_ap)
        q.ops.append((waits, fn, (k, 16)))
        self._book((k, v), reads, writes, nowaw)

    def collective(self, in_buf, out_buf, in_ap, out_ap):
        s = self.streams["pool"]
        k = "cc%d" % self.ncc
        self.ncc += 1
        self.sems[k] = self.es.enter_context(self.nc.semaphore(k))
        waits = self._waits(s, [in_buf], [out_buf], False)

        def fn(e):
            return e.collective_compute("AllGather", ALU.bypass,
                                        replica_groups=[list(range(NCORES))],
                                        ins=[in_ap], outs=[out_ap])
        s.ops.append((waits, fn, (k, 1)))
        self._book((k, 1), [in_buf], [out_buf], False)

    def barrier(self):
        tot = {}
        for name, s in self.streams.items():
            if name in ("pe", "act", "dve", "pool"):
                tot[name] = s.count
            if name in ("sp", "pool"):
                for i in range(min(ND, s.ndma)):
                    n_i = (s.ndma - 1 - i) // ND + 1
                    tot["%s_d%d" % (name, i)] = 16 * n_i
        for i in range(self.ncc):
            tot["cc%d" % i] = 1
        for name, s in self.streams.items():
            waits = []
            for k, v in tot.items():
                if v > 0 and s.seen.get(k, 0) < v and not (k == name and name == "pe"):
                    waits.append((k, v))
                    s.seen[k] = v
            if waits:
                s.ops.append((waits, None, None))

    def emit(self):
        nc = self.nc
        self.barrier()
        sems = self.sems

        def run(s, e):
            for waits, fn, inc in s.ops:
                for k, v in waits:
                    e.wait_ge(sems[k], v)
                if fn is not None:
                    ins = fn(e)
                    ins.then_inc(sems[inc[0]], inc[1])

        with nc.Block() as block:
            @block.sync
            def _(e):
                run(self.streams["sp"], e)

            @block.tensor
            def _(e):
                run(self.streams["pe"], e)

            @block.scalar
            def _(e):
                run(self.streams["act"], e)

            @block.vector
            def _(e):
                run(self.streams["dve"], e)

            @block.gpsimd
            def _(e):
                run(self.streams["pool"], e)


def build(debug=False):
    nc = bass.Bass("TRN2", target_bir_lowering=False)
    es = ExitStack()
    P = Prog(nc, es)

    def din(name, shape, dt=F32):
        return Buf(nc.dram_tensor(name, list(shape), dt, kind="ExternalInput").ap())

    def dout(name, shape, dt=F32):
        return Buf(nc.dram_tensor(name, list(shape), dt, kind="ExternalOutput").ap())

    def dint(name, shape, dt):
        return Buf(nc.dram_tensor(name, list(shape), dt, kind="Internal").ap())

    x_d = din("x", [NR, 128, D])
    cosn_d = din("cosn", [128, NR, 8])
    sinn_d = din("sinn", [128, NR, 8])
    cosr_d = din("cosr", [128, NR, 64])
    sinr_d = din("sinr", [128, NR, 64])
    w_in_d = din("w_in", [D, PT])
    norm_pre_d = din("norm_pre", [D])
    identb_d = din("identb", [128, 128], BF16)
    identf_d = din("identf", [128, 128])
    kdec_d = din("kdec", [128, 4])
    y_d = dout("y", [NR, 128, D])
    xall_d = din("x_all", [128, 128, D])
    cosna_d = din("cosn_all", [128, 128, 8])
    sinna_d = din("sinn_all", [128, 128, 8])
    cosra_d = din("cosr_all", [128, 128, 64])
    sinra_d = din("sinr_all", [128, 128, 64])
    wb_d = din("wb", [128, 2, 12, 128], BF16)
    db_d = din("db", [128, 2, 8, 128], BF16)
    cb_d = din("cb", [128, 2, 72])
    e32_d = din("e32", [128, 32, 128], BF16)
    iota_d = din("iota", [128, 256])
    cur_d = din("cur", [128, NR, 2])
    qdec_d = din("qdec", [128, 4])
    dmask_d = din("dmask", [128, 4, 128])
    wret_d = din("wret", [128, NR, 16, 4])
    dfac_d = din("dfac", [128, NR, 4])
    bgate_d = din("b_nsa_gate", [24])
    npost_d = din("norm_post", [D])
    wno_d = din("w_nsa_o", [512, D])
    wro_d = din("w_ret_o", [D, D])
    wo_d = din("w_out", [D, D])
    lab = dint("lab", [NR * 128, 1536], BF16)
    w1k_d = din("cmp_w1_k", [2048, 256])
    w2k_d = din("cmp_w2_k", [256, 64])
    pek_d = din("cmp_pe_k", [32, 64])
    w1v_d = din("cmp_w1_v", [2048, 256])
    w2v_d = din("cmp_w2_v", [256, 64])
    pev_d = din("cmp_pe_v", [32, 64])

    g4_in = dint("g4_in", [NR * 128, 512], BF16)
    g4 = dint("g4", [NCORES * NR * 128, 512], BF16)
    gv_in = dint("gv_in", [NR * 128, 264], BF16)
    gv = dint("gv", [NCORES * NR * 128, 264], BF16)
    gkv_in = dint("gkv_in", [NR * 128, 1024], BF16)
    gkv = dint("gkv", [NCORES * NR * 128, 1024], BF16)
    lkv = dint("lkv", [NR * 128, 1536], BF16)

    dbg = {}
    if debug:
        dbg["g4"] = dout("dbg_g4", [NCORES * NR * 128, 512], BF16)
        dbg["gv"] = dout("dbg_gv", [NCORES * NR * 128, 264], BF16)
        dbg["gkv"] = dout("dbg_gkv", [NCORES * NR * 128, 1024], BF16)
        dbg["lkv"] = dout("dbg_lkv", [NR * 128, 1536], BF16)
        dbg["kcT"] = dout("dbg_kcT", [128, 1024], BF16)
        dbg["lab"] = dout("dbg_lab", [NR * 128, 1536], BF16)
        dbg["vcS"] = dout("dbg_vcS", [128, 8, 128], BF16)

    identb = P.sb("identb", [128, 128], BF16)
    identf = P.sb("identf", [128, 128], F32)
    gcol = P.sb("gcol", [128, 8], F32)
    kdec = P.sb("kdec", [128, 4], F32)
    cosN = P.sb("cosN", [128, NR, 8], F32)
    sinN = P.sb("sinN", [128, NR, 8], F32)
    cosR = P.sb("cosR", [128, NR, 64], F32)
    sinR = P.sb("sinR", [128, NR, 64], F32)

    ps_a = P.ps("ps_a", [128, 512], F32)
    ps_b = P.ps("ps_b", [128, 512], F32)
    ps_c = P.ps("ps_c", [128, 1024], F32)
    ps_c0 = Buf(ps_c.t, excl=True)
    ps_c1 = Buf(ps_c.t, excl=True)
    ps_d = P.ps("ps_d", [128, 512], F32)
    ps_e = P.ps("ps_e", [128, 512], F32)
    ps_p = P.ps("ps_p", [128, 512], F32)
    ps_t = P.ps("ps_t", [128, 1024], BF16)

    P.dma("sp", identb[:], identb_d[:], writes=[identb])
    P.dma("sp", identf[:], identf_d[:], writes=[identf])
    P.dma("sp", gcol[:], norm_pre_d.t.rearrange("(c p) -> p c", p=128), writes=[gcol], slow=True)
    P.dma("sp", kdec[:], kdec_d[:], writes=[kdec])

    P.dma("sp", cosN[:], cosn_d[:], writes=[cosN])
    P.dma("sp", sinN[:], sinn_d[:], writes=[sinN])
    P.dma("sp", cosR[:], cosr_d[:], writes=[cosR])
    P.dma("sp", sinR[:], sinr_d[:], writes=[sinR])

    es1 = ExitStack()
    P1 = Prog.__new__(Prog)
    P1.__dict__ = P.__dict__.copy()
    P1.es = es1
    NKV = 768 + 1536
    wkv = P1.sb("wkv", [128, 8, NKV], BF16)
    stg = [P1.sb("wstg%d" % i, [128, 1536], F32) for i in range(2)]
    cast_eng = ["dve", "pool", "act"]
    ci = 0
    for kc in range(8):
        for (c0, n, o0) in ((C_KC, 768, 0), (C_KR, 1536, 768)):
            st = stg[ci % 2]
            P.dma("sp", st[:, 0:n], w_in_d[kc * 128:(kc + 1) * 128, c0:c0 + n], writes=[st])
            en = cast_eng[ci % 3]
            if en == "act":
                P.op("act", lambda e, st=st, n=n, kc=kc, o0=o0: e.activation(out=wkv[:, kc, o0:o0 + n], in_=st[:, 0:n], func=AF.Copy),
                     reads=[st], writes=[wkv], nowaw=True)
            else:
                P.op(en, lambda e, st=st, n=n, kc=kc, o0=o0: e.tensor_copy(out=wkv[:, kc, o0:o0 + n], in_=st[:, 0:n]),
                     reads=[st], writes=[wkv], nowaw=True)
            ci += 1

    xt = [P1.sb("xt%d" % i, [128, D], F32) for i in range(2)]
    junk = P1.sb("junk", [128, D], BF16)
    ss = P1.sb("ss", [128, 1], F32)
    rstd = P1.sb("rstd", [128, 1], F32)
    xn = P1.sb("xn", [128, D], BF16)
    hT = P1.sb("hT", [128, 8, 128], BF16)
    nsa_tm = P1.sb("nsa_tm", [128, 768], BF16)
    pf = P1.sb("pf", [128, 768], F32)
    rt = [P1.sb("rt%d" % i, [128, 4 * 64], F32) for i in range(4)]
    k4 = [P1.sb("k4_%d" % i, [128, 512], BF16) for i in range(2)]
    vaug = [P1.sb("vaug%d" % i, [128, 2, 2, 66], BF16) for i in range(2)]
    krvr = [P1.sb("krvr%d" % i, [128, 1536], BF16) for i in range(2)]
    kd = P1.sb("kd", [128, 512], BF16)
    kvm = [P1.sb("kvm%d" % i, [128, 1024], BF16) for i in range(2)]
    for i in range(2):
        P.op("pool", lambda e, i=i: e.memset(vaug[i][:], 1.0), writes=[vaug[i]])

    W = dict(junk=junk, ss=ss, rstd=rstd, xn=xn, hT=hT, rt=rt)

    def rms_front(xtile, gvec_col, hT_out):
        junk, ss, rstd, xn = W["junk"], W["ss"], W["rstd"], W["xn"]
        P.op("act", lambda e: e.activation(out=junk[:], in_=xtile[:], func=AF.Square, accum_out=ss[:]),
             reads=[xtile], writes=[junk, ss])
        P.op("dve", lambda e: e.tensor_scalar(out=rstd[:], in0=ss[:], scalar1=1.0 / D, scalar2=1e-6,
                                              op0=ALU.mult, op1=ALU.add), reads=[ss], writes=[rstd])
        P.op("act", lambda e: e.activation(out=rstd[:], in_=rstd[:], func=AF.Ln), reads=[rstd], writes=[rstd])
        P.op("act", lambda e: e.activation(out=rstd[:], in_=rstd[:], func=AF.Exp, scale=-0.5), reads=[rstd], writes=[rstd])
        P.op("dve", lambda e: e.tensor_scalar(out=xn[:], in0=xtile[:], scalar1=rstd[:], scalar2=None, op0=ALU.mult),
             reads=[xtile, rstd], writes=[xn])
        for c in range(8):
            P.op("pe", lambda e, c=c: e.transpose(out=ps_t[:, c * 128:(c + 1) * 128], in_=xn[:, c * 128:(c + 1) * 128],
                                                  identity=identb[:]),
                 reads=[xn, identb], writes=[ps_t], nowaw=(c > 0))
        P.op("dve", lambda e: e.tensor_tensor(out=hT_out[:], in0=ps_t[:].rearrange("p (c t) -> p c t", c=8),
                                              in1=gvec_col[:].unsqueeze(2).to_broadcast([128, 8, 128]), op=ALU.mult),
             reads=[ps_t, gvec_col], writes=[hT_out])

    def proj(ps, ncols, w, wc0):
        hT = W["hT"]
        for kc in range(8):
            P.op("pe", lambda e, kc=kc: e.matmul(ps[:, 0:ncols], lhsT=hT[:, kc, :], rhs=w[:, kc, wc0:wc0 + ncols],
                                                 start=(kc == 0), stop=(kc == 7)),
                 reads=[hT, w], writes=[ps], nowaw=(kc > 0))

    def rotary(src, dst, nh, hd, half, cos_r, sin_r, G3=None):
        sb_, sap = src
        db_, dap = dst
        rt = W["rt"]
        if G3 is None:
            x1 = sap[:, :, 0:half]
            x2 = sap[:, :, half:2 * half]
            d1 = dap[:, :, 0:half]
            d2 = dap[:, :, half:2 * half]
            shp = [128, nh, half]
            cb = cos_r.unsqueeze(1).to_broadcast(shp)
            sbb = sin_r.unsqueeze(1).to_broadcast(shp)
            tv = [t[:, 0:nh * half].rearrange("p (h d) -> p h d", h=nh) for t in rt]
        else:
            a, b = G3
            x1 = sap[:, :, :, 0:half]
            x2 = sap[:, :, :, half:2 * half]
            d1 = dap[:, :, :, 0:half]
            d2 = dap[:, :, :, half:2 * half]
            shp = [128, a, b, half]
            cb = cos_r.unsqueeze(1).unsqueeze(1).to_broadcast(shp)
            sbb = sin_r.unsqueeze(1).unsqueeze(1).to_broadcast(shp)
            tv = [t[:, 0:a * b * half].rearrange("p (a b d) -> p a b d", a=a, b=b) for t in rt]
        tbl = [cosR, sinR, cosN, sinN] + W.get("ropex", [])
        P.op("dve", lambda e: e.tensor_tensor(out=tv[0], in0=x1, in1=cb, op=ALU.mult), reads=[sb_] + tbl, writes=[rt[0]])
        P.op("dve", lambda e: e.tensor_tensor(out=tv[1], in0=x2, in1=sbb, op=ALU.mult), reads=[sb_] + tbl, writes=[rt[1]])
        P.op("dve", lambda e: e.tensor_tensor(out=tv[2], in0=x2, in1=cb, op=ALU.mult), reads=[sb_] + tbl, writes=[rt[2]])
        P.op("dve", lambda e: e.tensor_tensor(out=tv[3], in0=x1, in1=sbb, op=ALU.mult), reads=[sb_] + tbl, writes=[rt[3]])
        P.op("dve", lambda e: e.tensor_tensor(out=d1, in0=tv[0], in1=tv[1], op=ALU.subtract),
             reads=[rt[0], rt[1]], writes=[db_], nowaw=False)
        P.op("dve", lambda e: e.tensor_tensor(out=d2, in0=tv[2], in1=tv[3], op=ALU.add),
             reads=[rt[2], rt[3]], writes=[db_], nowaw=True)

    KST = int(os.environ.get("KSTAGE", "9"))
    cosNa = P1.sb("cosNa", [128, 128, 8], F32)
    sinNa = P1.sb("sinNa", [128, 128, 8], F32)
    crs = [P1.sb("crs%d" % i, [128, 64], F32) for i in range(2)]
    srs = [P1.sb("srs%d" % i, [128, 64], F32) for i in range(2)]
    P.dma("sp", cosNa[:], cosna_d[:], writes=[cosNa])
    P.dma("sp", sinNa[:], sinna_d[:], writes=[sinNa])
    W["ropex"] = [cosNa, sinNa] + crs + srs
    junkB = P1.sb("junkB", [128, D], BF16)
    ssB = P1.sb("ssB", [128, 1], F32)
    rstdB = P1.sb("rstdB", [128, 1], F32)
    xnB = P1.sb("xnB", [128, D], BF16)
    hTB = P1.sb("hTB", [128, 8, 128], BF16)
    pfB = P1.sb("pfB", [128, 768], F32)
    nsa_tmB = P1.sb("nsa_tmB", [128, 768], BF16)
    kdB = P1.sb("kdB", [128, 512], BF16)
    rtB = [P1.sb("rtB%d" % i, [128, 4 * 64], F32) for i in range(4)]
    Wpar = [dict(junk=junk, ss=ss, rstd=rstd, xn=xn, hT=hT, rt=rt), dict(junk=junkB, ss=ssB, rstd=rstdB, xn=xnB, hT=hTB, rt=rtB)]
    bufpar = [(hT, pf, nsa_tm, kd, ps_a, ps_b), (hTB, pfB, nsa_tmB, kdB, ps_e, ps_p)]

    def p1_tile(it, hT, pf, nsa_tm, kd, ps_a, ps_b, phase):
        own = it >= 128
        r = it - 128 if own else it
        xb = xt[it % 2]
        if own:
            if phase == 0:
                P.dma("sp", xb[:], x_d[r], writes=[xb])
            cN_ap = sN_ap = None
            cR_ap, sR_ap = cosR[:, r, :], sinR[:, r, :]
        else:
            if phase == 0:
                P.dma("sp", xb[:], xall_d[r], writes=[xb])
                P.dma("sp", crs[it % 2][:], cosra_d[r], writes=[crs[it % 2]])
                P.dma("sp", srs[it % 2][:], sinra_d[r], writes=[srs[it % 2]])
            cN_ap, sN_ap = cosNa[:, r, :], sinNa[:, r, :]
            cR_ap, sR_ap = crs[it % 2][:], srs[it % 2][:]
        KSUB = int(os.environ.get("KSUB", "9"))
        if phase == 0:
            rms_front(xb, gcol, hT)
            return
        if KSUB < 2:
            return
        nsa_on = not own
        if nsa_on:
          proj(ps_a, 512, wkv, 0)
          proj(ps_b, 256, wkv, 512)
          P.op("act", lambda e: e.activation(out=pf[:, 0:512], in_=ps_a[:, 0:512], func=AF.Copy),
               reads=[ps_a], writes=[pf])
          P.op("act", lambda e: e.activation(out=pf[:, 512:768], in_=ps_b[:, 0:256], func=AF.Copy),
               reads=[ps_b], writes=[pf], nowaw=True)
          P.op("dve", lambda e: e.tensor_copy(out=nsa_tm[:], in_=pf[:]), reads=[pf], writes=[nsa_tm])
          pf5 = pf[:, 0:768].rearrange("p (a x b d) -> p a x b d", a=3, x=2, b=2)[:, :, 0, :, :]
          nt5 = nsa_tm[:, 0:768].rearrange("p (a x b d) -> p a x b d", a=3, x=2, b=2)[:, :, 0, :, :]
          rotary((pf, pf5), (nsa_tm, nt5), None, 64, 8, cN_ap, sN_ap, G3=(3, 2))
          vb = vaug[it % 2]
          P.op("dve", lambda e, vb=vb: e.tensor_copy(out=vb[:, 0, :, 0:64],
                                                     in_=nsa_tm[:, 384:512].rearrange("p (g d) -> p g d", g=2)),
               reads=[nsa_tm], writes=[vb])
          P.op("dve", lambda e, vb=vb: e.tensor_copy(out=vb[:, 1, :, 0:64],
                                                     in_=nsa_tm[:, 640:768].rearrange("p (g d) -> p g d", g=2)),
               reads=[nsa_tm], writes=[vb], nowaw=True)
          P.dma("pool", gv[r * 128:(r + 1) * 128, :], vb[:].rearrange("p a g d -> p (a g d)"),
                reads=[vb], writes=[gv], nowaw=True)
        if KSUB < 6:
            return
        kb = krvr[it % 2]
        proj(ps_d, 512, wkv, 768)
        P.op("act", lambda e: e.activation(out=pf[:, 0:512], in_=ps_d[:, 0:512], func=AF.Copy),
             reads=[ps_d], writes=[pf])
        rotary((pf, pf[:, 0:512].rearrange("p (h d) -> p h d", h=4)),
               (kb, kb[:, 0:512].rearrange("p (h d) -> p h d", h=4)),
               4, 128, 64, cR_ap, sR_ap)
        proj(ps_a, 512, wkv, 768 + 512)
        proj(ps_b, 512, wkv, 768 + 1024)
        P.op("act", lambda e, kb=kb: e.activation(out=kb[:, 512:1024], in_=ps_a[:, 0:512], func=AF.Copy),
             reads=[ps_a], writes=[kb], nowaw=True)
        P.op("act", lambda e, kb=kb: e.activation(out=kb[:, 1024:1536], in_=ps_b[:, 0:512], func=AF.Copy),
             reads=[ps_b], writes=[kb], nowaw=True)
        if own:
            P.dma("pool", lkv[r * 128:(r + 1) * 128, :], kb[:], reads=[kb], writes=[lkv], nowaw=True)
            return
        if nsa_on:
          for i, c0 in enumerate((0, 128, 256, 512)):
              P.op("pe", lambda e, i=i, c0=c0: e.transpose(out=ps_t[:, i * 128:(i + 1) * 128], in_=nsa_tm[:, c0:c0 + 128],
                                                           identity=identb[:]),
                   reads=[nsa_tm, identb], writes=[ps_t], nowaw=(i > 0))
          k4b = k4[it % 2]
          P.op("act", lambda e, k4b=k4b: e.activation(out=k4b[:], in_=ps_t[:, 0:512], func=AF.Copy),
               reads=[ps_t], writes=[k4b])
          P.dma("pool", g4[r * 128:(r + 1) * 128, :], k4b[:], reads=[k4b], writes=[g4], nowaw=True)
        P.op("dve", lambda e, kb=kb: e.tensor_tensor(out=kd[:].rearrange("p (h d) -> p h d", h=4),
                                                     in0=kb[:, 0:512].rearrange("p (h d) -> p h d", h=4),
                                                     in1=kdec[:].unsqueeze(2).to_broadcast([128, 4, 128]), op=ALU.mult),
             reads=[kb, kdec], writes=[kd])
        for h in range(4):
            P.op("pe", lambda e, h=h, kb=kb: e.matmul(ps_c[:, h * 256:(h + 1) * 256], lhsT=kd[:, h * 128:(h + 1) * 128],
                                                      rhs=kb[:, 512 + h * 256:512 + (h + 1) * 256], start=True, stop=True),
                 reads=[kd, kb], writes=[ps_c0, ps_c1], nowaw=(h > 0))
        kvb = kvm[it % 2]
        P.op("act", lambda e, kvb=kvb: e.activation(out=kvb[:, 0:512], in_=ps_c[:, 0:512], func=AF.Copy),
             reads=[ps_c0, ps_c1], writes=[kvb])
        P.op("dve", lambda e, kvb=kvb: e.tensor_copy(out=kvb[:, 512:1024], in_=ps_c[:, 512:1024]),
             reads=[ps_c0, ps_c1], writes=[kvb], nowaw=True)
        P.dma("pool", gkv[r * 128:(r + 1) * 128, :], kvb[:], reads=[kvb], writes=[gkv], nowaw=True)


    n_p1 = (128 + NR) if KST >= 2 else 0
    if n_p1:
        W.update(Wpar[0])
        p1_tile(0, *bufpar[0], phase=0)
    for it in range(n_p1):
        if it + 1 < n_p1:
            W.update(Wpar[(it + 1) % 2])
            p1_tile(it + 1, *bufpar[(it + 1) % 2], phase=0)
        W.update(Wpar[it % 2])
        p1_tile(it, *bufpar[it % 2], phase=1)
    W.update(Wpar[0])
    P.barrier()
    es1.close()

    if debug and os.environ.get("KDUMP1", "0") == "1":
        P.dma("sp", dbg["g4"][:], g4[:], reads=[g4], writes=[dbg["g4"]])
        P.dma("sp", dbg["gv"][:], gv[:], reads=[gv], writes=[dbg["gv"]])
        P.dma("sp", dbg["gkv"][:], gkv[:], reads=[gkv], writes=[dbg["gkv"]])
        P.dma("sp", dbg["lkv"][:], lkv[:], reads=[lkv], writes=[dbg["lkv"]])

    kcT = P.sb("kcT", [128, 1024], BF16)
    vcS = P.sb("vcS", [128, 8, 128], BF16)
    es2 = ExitStack()
    P2 = Prog.__new__(Prog)
    P2.__dict__ = P.__dict__.copy()
    P2.es = es2
    KVT = P2.sb("KVT", [128, S], BF16)
    w1T = P2.sb("w1T", [128, 32, 256], BF16)
    w1stg = P2.sb("w1stg", [128, 8, 256], F32)
    w2stg = P2.sb("w2stg", [128, 2, 64], F32)
    w2pad = P2.sb("w2pad", [128, 2, 2, 128], BF16)
    w2s = P2.sb("w2s", [128, 2, 64], BF16)
    pestg = P2.sb("pestg", [128, 32], F32)
    peT = P2.sb("peT", [128, 32], BF16)
    b1 = P2.sb("b1", [128, 2], F32)
    h1T = P2.sb("h1T", [128, 2, 2, 1024], BF16)
    vcT = P2.sb("vcT", [128, 1024], BF16)
    zt = P2.sb("zt", [128, 512], F32)
    tt = P2.sb("tt", [128, 512], F32)
    et = P2.sb("et", [128, 512], F32)
    P.op("pool", lambda e: e.memset(h1T[:], 0.0), writes=[h1T])
    P.op("pool", lambda e: e.memset(w2pad[:], 0.0), writes=[w2pad])
    GC = float(2.0 * np.sqrt(2.0 / np.pi))
    for kv_i, (w1_d, w2_d, pe_d) in enumerate(((w1k_d, w2k_d, pek_d), (w1v_d, w2v_d, pev_d))):
        for b in range(128):
            sl = slot_of(b)
            P.dma("sp", KVT[:, b * 128:(b + 1) * 128], g4[sl * 128:(sl + 1) * 128, kv_i * 128:(kv_i + 1) * 128],
                  reads=[g4], writes=[KVT], nowaw=(b > 0))
        w1v = w1_d.t.rearrange("(l d) h -> d l h", d=64)
        for part in range(4):
            for half in range(2):
                P.dma("sp", w1stg[half * 64:(half + 1) * 64, :, :], w1v[:, part * 8:(part + 1) * 8, :],
                      writes=[w1stg], nowaw=(half > 0))
            P.op("dve", lambda e, part=part: e.tensor_copy(out=w1T[:, part * 8:(part + 1) * 8, :], in_=w1stg[:]),
                 reads=[w1stg], writes=[w1T], nowaw=(part > 0))
        P.dma("sp", w2stg[:], w2_d.t.rearrange("(hh p) d -> p hh d", p=128), writes=[w2stg])
        P.op("dve", lambda e: e.tensor_copy(out=w2s[:], in_=w2stg[:]), reads=[w2stg], writes=[w2s])
        for g in range(2):
            P.op("dve", lambda e, g=g: e.tensor_copy(out=w2pad[:, :, g, g * 64:(g + 1) * 64], in_=w2stg[:]),
                 reads=[w2stg], writes=[w2pad], nowaw=(g > 0))
        for half in range(2):
            P.dma("sp", pestg[half * 64:(half + 1) * 64, :], pe_d.t.rearrange("l d -> d l"), writes=[pestg],
                  nowaw=(half > 0), slow=True)
        P.op("dve", lambda e: e.tensor_copy(out=peT[:], in_=pestg[:]), reads=[pestg], writes=[peT])
        KCMP = int(os.environ.get("KCMP", "9"))
        if KCMP < 2:
            continue
        for hh in range(2):
            for l in range(32):
                P.op("pe", lambda e, hh=hh, l=l: e.matmul(ps_p[:, hh:hh + 1], lhsT=w1T[0:64, l, hh * 128:(hh + 1) * 128],
                                                          rhs=peT[0:64, l:l + 1], start=(l == 0), stop=(l == 31)),
                     reads=[w1T, peT], writes=[ps_p], nowaw=not (hh == 0 and l == 0))
        P.op("dve", lambda e: e.tensor_copy(out=b1[:], in_=ps_p[:, 0:2]), reads=[ps_p], writes=[b1])
        if KCMP < 3:
            continue
        KV3 = KVT[:].rearrange("p (c s) -> p c s", s=16)
        pss = [ps_a, ps_b]
        it = 0
        for g in range(2):
            for hh in range(2):
                for ch in range(2):
                    c0 = 512 * ch
                    n = 512 if ch == 0 else 511
                    psx = pss[it % 2]
                    it += 1
                    for l in range(32):
                        cs = c0 + (l // 16)
                        P.op("pe", lambda e, g=g, hh=hh, l=l, cs=cs, n=n, psx=psx: e.matmul(
                            psx[:, 0:n], lhsT=w1T[g * 64:(g + 1) * 64, l, hh * 128:(hh + 1) * 128],
                            rhs=KV3[g * 64:(g + 1) * 64, cs:cs + n, l % 16], start=(l == 0), stop=(l == 31)),
                            reads=[w1T, KVT], writes=[psx], nowaw=(l > 0))
                    if KCMP < 4:
                        continue
                    P.op("dve", lambda e, psx=psx, n=n, hh=hh: e.tensor_scalar(out=zt[:, 0:n], in0=psx[:, 0:n], scalar1=b1[:, hh:hh + 1],
                                                                                scalar2=None, op0=ALU.add),
                         reads=[psx, b1], writes=[zt])
                    P.op("pool", lambda e, n=n: e.tensor_tensor(out=tt[:, 0:n], in0=zt[:, 0:n], in1=zt[:, 0:n], op=ALU.mult),
                         reads=[zt], writes=[tt])
                    P.op("pool", lambda e, n=n: e.tensor_scalar(out=tt[:, 0:n], in0=tt[:, 0:n], scalar1=0.044715, scalar2=1.0,
                                                                op0=ALU.mult, op1=ALU.add), reads=[tt], writes=[tt])
                    P.op("pool", lambda e, n=n: e.tensor_tensor(out=tt[:, 0:n], in0=tt[:, 0:n], in1=zt[:, 0:n], op=ALU.mult),
                         reads=[tt, zt], writes=[tt])
                    P.op("act", lambda e, n=n: e.activation(out=et[:, 0:n], in_=tt[:, 0:n], func=AF.Exp, scale=-GC),
                         reads=[tt], writes=[et])
                    P.op("dve", lambda e, n=n: e.tensor_scalar(out=et[:, 0:n], in0=et[:, 0:n], scalar1=1.0, scalar2=None, op0=ALU.add),
                         reads=[et], writes=[et])
                    P.op("dve", lambda e, n=n: e.reciprocal(out=et[:, 0:n], in_=et[:, 0:n]), reads=[et], writes=[et])
                    P.op("dve", lambda e, n=n, hh=hh, g=g, c0=c0: e.tensor_tensor(out=h1T[:, hh, g, c0:c0 + n], in0=zt[:, 0:n], in1=et[:, 0:n],
                                                                                 op=ALU.mult),
                         reads=[zt, et], writes=[h1T], nowaw=True)
        if KCMP < 5:
            continue
        dstT = kcT if kv_i == 0 else vcT
        for ch in range(2):
            c0 = 512 * ch
            k = 0
            for g in range(2):
                for hh in range(2):
                    P.op("pe", lambda e, g=g, hh=hh, c0=c0, k=k: e.matmul(ps_d[:, 0:512], lhsT=w2pad[:, hh, g, :],
                                                                         rhs=h1T[:, hh, g, c0:c0 + 512], start=(k == 0), stop=(k == 3)),
                         reads=[w2pad, h1T], writes=[ps_d], nowaw=(k > 0))
                    k += 1
            P.op("act", lambda e, c0=c0, dstT=dstT: e.activation(out=dstT[:, c0:c0 + 512], in_=ps_d[:, 0:512], func=AF.Copy),
                 reads=[ps_d], writes=[dstT], nowaw=(ch > 0))
        if kv_i == 1:
            for cc in range(8):
                P.op("pe", lambda e, cc=cc: e.transpose(out=ps_t[:, cc * 128:(cc + 1) * 128], in_=vcT[:, cc * 128:(cc + 1) * 128],
                                                        identity=identb[:]),
                     reads=[vcT, identb], writes=[ps_t], nowaw=(cc > 0))
            P.op("dve", lambda e: e.tensor_copy(out=vcS[:], in_=ps_t[:].rearrange("p (c f) -> p c f", c=8)),
                 reads=[ps_t], writes=[vcS])
    P.barrier()
    es2.close()
    if debug:
        P.dma("sp", dbg["kcT"][:], kcT[:], reads=[kcT], writes=[dbg["kcT"]])
        P.dma("sp", dbg["vcS"][:], vcS[:], reads=[vcS], writes=[dbg["vcS"]])

    es3 = ExitStack()
    P3 = Prog.__new__(Prog)
    P3.__dict__ = P.__dict__.copy()
    P3.es = es3
    KP2 = int(os.environ.get("KP2", "99"))
    xt3 = P3.sb("xt3", [128, D], F32)
    W["ropex"] = []
    W.update(junk=P3.sb("junk3", [128, D], BF16), ss=P3.sb("ss3", [128, 1], F32), rstd=P3.sb("rstd3", [128, 1], F32),
             xn=P3.sb("xn3", [128, D], BF16), hT=P3.sb("hT3", [128, 8, 128], BF16),
             rt=[P3.sb("rt3_%d" % i, [128, 256], F32) for i in range(4)])
    hT = W["hT"]
    pf2a = P3.sb("pf3", [128, 512], F32)
    QN, GN, ZN, QR, ZR = 0, 512, 536, 1048, 1560
    wq = P3.sb("wq", [128, 8, 2584], BF16)
    stg2 = P3.sb("wstgb", [128, 1048], F32)
    ci = 0
    for kc in range(8):
        for (c0, n, o0) in ((0, 512, 0), (1280, 1048, 512), (3864, 1024, 1560)):
            P.dma("sp", stg2[:, 0:n], w_in_d[kc * 128:(kc + 1) * 128, c0:c0 + n], writes=[stg2])
            en = cast_eng[ci % 3]
            if en == "act":
                P.op("act", lambda e, n=n, kc=kc, o0=o0: e.activation(out=wq[:, kc, o0:o0 + n], in_=stg2[:, 0:n], func=AF.Copy),
                     reads=[stg2], writes=[wq], nowaw=True)
            else:
                P.op(en, lambda e, n=n, kc=kc, o0=o0: e.tensor_copy(out=wq[:, kc, o0:o0 + n], in_=stg2[:, 0:n]),
                     reads=[stg2], writes=[wq], nowaw=True)
            ci += 1
    e32 = P3.sb("e32", [128, 32, 128], BF16)
    wb = P3.sb("wb", [128, 2, 12, 128], BF16)
    dbm = P3.sb("dbm", [128, 2, 8, 128], BF16)
    cb = P3.sb("cb", [128, 2, 72], F32)
    iota = P3.sb("iota", [128, 256], F32)
    cur = P3.sb("cur", [128, NR, 2], F32)
    qdec = P3.sb("qdec", [128, 4], F32)
    dmask = P3.sb("dmask", [128, 4, 128], F32)
    wret = P3.sb("wret", [128, NR, 16, 4], F32)
    dfac = P3.sb("dfac", [128, NR, 4], F32)
    bgate = P3.sb("bgate", [128, 24], F32)
    for t, d in ((e32, e32_d), (wb, wb_d), (dbm, db_d), (cb, cb_d), (iota, iota_d), (cur, cur_d), (qdec, qdec_d),
                 (dmask, dmask_d), (wret, wret_d), (dfac, dfac_d)):
        P.dma("sp", t[:], d[:], writes=[t])
    P.dma("sp", bgate[:], bgate_d.t.partition_broadcast(128), writes=[bgate])

    Rst = P3.sb("Rst", [128, 1024], F32)
    Pc = P3.sb("Pc", [128, 1024], BF16)
    PcT = P3.sb("PcT", [128, 1024], BF16)
    pg = [P3.sb("pg%d" % g, [128, 1024], F32) for g in range(2)]
    P.op("pool", lambda e: e.memset(Rst[:], 0.0), writes=[Rst])
    P.op("pool", lambda e: e.memset(Pc[:], 0.0), writes=[Pc])
    for g in range(2):
        P.op("pool", lambda e, g=g: e.memset(pg[g][:], 0.0), writes=[pg[g]])
    qn_tm = P3.sb("qn_tm", [128, 512], BF16)
    qT = P3.sb("qT", [128, 4, 128], BF16)
    gt = P3.sb("gt", [128, 24], F32)
    ez = P3.sb("ez", [128, 1024], F32)
    szn = P3.sb("szn", [128, 512], F32)
    oa = P3.sb("oa", [128, 8, 64], F32)
    rs = P3.sb("rs", [128, 8], F32)
    rinv = P3.sb("rinv", [128, 8], F32)
    sc8 = P3.sb("sc8", [128, 8], F32)
    m1 = P3.sb("m1", [128, 256], F32)
    m2 = P3.sb("m2", [128, 256], F32)
    ta = P3.sb("ta", [128, 256], F32)
    s4 = P3.sb("s4", [128, 256], F32)
    score = P3.sb("score", [128, 256], F32)
    rep = P3.sb("rep", [128, 256], F32)
    mx = P3.sb("mx", [128, 16], F32)
    thr = P3.sb("thr", [128, 1], F32)
    biastm = P3.sb("biastm", [128, 256], BF16)
    BT8 = [[P3.sb("BT8_%d_%d" % (g, t), [128, 2, 128], BF16) for t in range(2)] for g in range(2)]
    for g in range(2):
        for t in range(2):
            P.op("pool", lambda e, g=g, t=t: e.memset(BT8[g][t][:], 0.0), writes=[BT8[g][t]])
    K8 = [[P3.sb("K8_%d_%d" % (i, g), [128, 8, 128], BF16) for g in range(2)] for i in range(2)]
    for i in range(2):
        for g in range(2):
            P.op("pool", lambda e, i=i, g=g: e.memset(K8[i][g][:], 0.0), writes=[K8[i][g]])
    V8 = [P3.sb("V8_%d" % i, [128, 8, 132], BF16) for i in range(2)]
    eT = [P3.sb("eT%d" % i, [128, 512], BF16) for i in range(4)]
    accsb = P3.sb("accsb", [128, 512], F32)
    den4 = P3.sb("den4", [128, 4], F32)
    s4g = P3.sb("s4g", [128, 4], F32)
    A1 = P3.sb("A1", [128, 512], BF16)
    qr = P3.sb("qr", [128, 512], BF16)
    qd = P3.sb("qd", [128, 512], BF16)
    krvr2 = P3.sb("krvr2", [128, 1536], BF16)
    qkT = P3.sb("qkT", [128, 8, 128], BF16)
    qdT = P3.sb("qdT", [128, 4, 128], BF16)
    attT = P3.sb("attT", [128, 4, 128], BF16)
    kvb = [P3.sb("kvb%d" % i, [128, 1024], BF16) for i in range(2)]
    tmpkv = P3.sb("tmpkv", [128, 1024], F32)
    Rb = P3.sb("Rb", [128, 1024], BF16)
    ob = P3.sb("ob", [128, 1024], F32)
    sm4 = P3.sb("sm4", [128, 4], F32)
    B1 = P3.sb("B1", [128, 1024], BF16)

    g4v = g4.t.rearrange("(c rr p) f -> p c rr f", c=8, rr=NR, p=128)
    gvv = gv.t.rearrange("(c rr p) f -> p c rr f", c=8, rr=NR, p=128)
    gate3 = gt[:].rearrange("p (h t) -> p h t", t=3)
    pss = [ps_a, ps_b]
    acc = [ps_d, ps_e]
    psl = [(ps_a, ps_a[:, 0:512]), (ps_b, ps_b[:, 0:512]), (ps_c0, ps_c[:, 0:512]), (ps_c1, ps_c[:, 512:1024])]

    def attn_branch(r, chunks, kcol, vcol, br):
        par = r % 2
        Rs = sorted(set(kk // 8 for kk, _ in chunks))
        steps = []
        for n_, R in enumerate(Rs):
            kb_, vb_ = K8[n_ % 2], V8[n_ % 2]
            for kk, bf_ in [(kk, bf_) for kk, bf_ in chunks if kk // 8 == R]:
                j = kk % 8
                c = j if R % 2 == 0 else 7 - j
                first = (kk == chunks[0][0])
                last = (kk == chunks[-1][0])
                for g in range(2):
                    extra = []
                    if br == 1:
                        t2_ = (kk % 64) // 32
                        extra.append((e32, e32[:, kk % 32, :], BT8[g][t2_],
                                      BT8[g][t2_][:, kk // 64, :].unsqueeze(1).to_broadcast([128, 4, 128])))
                    if bf_ is not None:
                        extra.append(bf_)
                    si = len(steps)
                    steps.append(dict(R=R, kb_=kb_, vb_=vb_, c=c, g=g, extra=extra, first=first, last=last,
                                      psx=psl[si % 4], et_=eT[si % 4]))
        loaded = set()

        def load_group(Rl):
            if Rl in loaded or Rl not in Rs:
                return
            loaded.add(Rl)
            nl = Rs.index(Rl)
            kbl, vbl = K8[nl % 2], V8[nl % 2]
            for g2 in range(2):
                P.dma("sp", kbl[g2][g2 * 64:(g2 + 1) * 64, :, :], g4v[g2 * 64:(g2 + 1) * 64, :, Rl, kcol:kcol + 128],
                      reads=[g4], writes=[kbl[g2]])
            P.dma("sp", vbl[:], gvv[:, :, Rl, vcol:vcol + 132], reads=[gv], writes=[vbl])

        def emit_scores(st):
            R, kb_, vb_, c, g, extra, psx, et_ = st["R"], st["kb_"], st["vb_"], st["c"], st["g"], st["extra"], st["psx"], st["et_"]
            load_group(R)
            psb, psap = psx
            P.op("pe", lambda e, psap=psap, kb_=kb_, c=c, g=g, ne=len(extra): e.matmul(
                psap, lhsT=kb_[g][:, c, :], rhs=qT[:, :, :],
                start=True, stop=(ne == 0)), reads=[kb_[g], qT], writes=[psb])
            for xi, (lb, lap, rb_, rap) in enumerate(extra):
                P.op("pe", lambda e, psap=psap, lap=lap, rap=rap, lst=(xi == len(extra) - 1): e.matmul(
                    psap, lhsT=lap, rhs=rap, start=False, stop=lst), reads=[lb, rb_], writes=[psb], nowaw=True)
            P.op("act", lambda e, psap=psap, et_=et_: e.activation(out=et_[:], in_=psap, func=AF.Exp),
                 reads=[psb], writes=[et_])

        def emit_pv(st):
            g, vb_, c, et_, first, last = st["g"], st["vb_"], st["c"], st["et_"], st["first"], st["last"]
            P.op("pe", lambda e, g=g, vb_=vb_, c=c, et_=et_, first=first, last=last: e.matmul(
                acc[g][0:65, 0:512], lhsT=vb_[:, c, g * 66:g * 66 + 65], rhs=et_[:], start=first, stop=last),
                reads=[vb_, et_], writes=[acc[g]], nowaw=not first)

        AHEAD = 2
        for si in range(min(AHEAD, len(steps))):
            emit_scores(steps[si])
        for si in range(len(steps)):
            if si + AHEAD < len(steps):
                emit_scores(steps[si + AHEAD])
            emit_pv(steps[si])
            if si == 0 or steps[si - 1]["R"] != steps[si]["R"]:
                load_group(steps[si]["R"] + 1)
        KAB = int(os.environ.get("KAB", "9"))
        for g in range(2 if KAB >= 2 else 0):
            P.op("act", lambda e, g=g: e.activation(out=accsb[0:65, :], in_=acc[g][0:65, 0:512], func=AF.Copy),
                 reads=[acc[g]], writes=[accsb])
            for hh in range(4):
                P.op("pe", lambda e, hh=hh: e.transpose(out=ps_p[:, hh * 65:(hh + 1) * 65], in_=accsb[0:65, hh * 128:(hh + 1) * 128],
                                                        identity=identf[0:65, 0:65]),
                     reads=[accsb, identf], writes=[ps_p], nowaw=(hh > 0))
            pv = ps_p[:, 0:260].rearrange("p (h f) -> p h f", f=65)
            P.op("dve", lambda e, pv=pv: e.tensor_scalar(out=den4[:], in0=pv[:, :, 64], scalar1=1e-30, scalar2=None, op0=ALU.max),
                 reads=[ps_p], writes=[den4])
            P.op("dve", lambda e: e.reciprocal(out=den4[:], in_=den4[:]), reads=[den4], writes=[den4])
            P.op("dve", lambda e, g=g: e.tensor_tensor(out=s4g[:], in0=den4[:], in1=gate3[:, g * 4:(g + 1) * 4, br], op=ALU.mult),
                 reads=[den4, gt], writes=[s4g])
            for hh in range(4):
                P.op("dve", lambda e, g=g, hh=hh: e.scalar_tensor_tensor(out=oa[:, g * 4 + hh, :], in0=ps_p[:, hh * 65:hh * 65 + 64],
                                                                         scalar=s4g[:, hh:hh + 1], in1=oa[:, g * 4 + hh, :],
                                                                         op0=ALU.mult, op1=ALU.add),
                     reads=[ps_p, s4g, oa], writes=[oa])

    KR = int(os.environ.get("KR", "99"))
    for r in range(min(NR, KR) if KP2 >= 1 else 0):
        par = r % 2
        P.dma("sp", xt3[:], x_d[r], writes=[xt3])
        rms_front(xt3, gcol, hT)
        Rst3 = Rst[:].rearrange("p (h e) -> p h e", h=4)
        P.op("pool", lambda e, r=r: e.tensor_tensor(out=Rst3, in0=Rst3, in1=dfac[:, r, :].unsqueeze(2).to_broadcast([128, 4, 256]), op=ALU.mult),
             reads=[Rst, dfac], writes=[Rst])
        nk = 0
        for mi in range(16):
            m = 8 * (r - 1) + mi
            if m < 0:
                continue
            kb2 = kvb[nk % 2]
            nk += 1
            sl = slot_of(m)
            P.dma("sp", kb2[:], gkv[sl * 128:(sl + 1) * 128, :], reads=[gkv], writes=[kb2])
            P.op("pool", lambda e, kb2=kb2, mi=mi, r=r: e.tensor_tensor(out=tmpkv[:].rearrange("p (h e) -> p h e", h=4),
                                                                   in0=kb2[:].rearrange("p (h e) -> p h e", h=4),
                                                                   in1=wret[:, r, mi, :].unsqueeze(2).to_broadcast([128, 4, 256]), op=ALU.mult),
                 reads=[kb2, wret], writes=[tmpkv])
            P.op("pool", lambda e: e.tensor_tensor(out=Rst[:], in0=Rst[:], in1=tmpkv[:], op=ALU.add), reads=[Rst, tmpkv], writes=[Rst])
        P.op("dve", lambda e: e.tensor_copy(out=Rb[:], in_=Rst[:]), reads=[Rst], writes=[Rb])
        proj(ps_a, 512, wq, QN)
        P.op("act", lambda e: e.activation(out=pf2a[:, 0:512], in_=ps_a[:, 0:512], func=AF.Copy, scale=0.125),
             reads=[ps_a], writes=[pf2a])
        pf4 = pf2a[:, 0:512].rearrange("p (g hh d) -> p g hh d", g=2, hh=4)
        qn4 = qn_tm[:].rearrange("p (hh g d) -> p g hh d", hh=4, g=2)
        P.op("dve", lambda e: e.tensor_copy(out=qn4, in_=pf4), reads=[pf2a], writes=[qn_tm])
        rotary((pf2a, pf4), (qn_tm, qn4), None, 64, 8, cosN[:, r, :], sinN[:, r, :], G3=(2, 4))
        for hh in range(4):
            P.op("pe", lambda e, hh=hh: e.transpose(out=ps_t[:, hh * 128:(hh + 1) * 128], in_=qn_tm[:, hh * 128:(hh + 1) * 128],
                                                    identity=identb[:]),
                 reads=[qn_tm, identb], writes=[ps_t], nowaw=(hh > 0))
        P.op("dve", lambda e: e.tensor_copy(out=qT[:], in_=ps_t[:, 0:512].rearrange("p (h t) -> p h t", h=4)),
             reads=[ps_t], writes=[qT])
        proj(ps_b, 24, wq, GN)
        P.op("dve", lambda e: e.tensor_tensor(out=gt[:], in0=ps_b[:, 0:24], in1=bgate[:], op=ALU.add),
             reads=[ps_b, bgate], writes=[gt])
        P.op("act", lambda e: e.activation(out=gt[:], in_=gt[:], func=AF.Exp, scale=-1.0), reads=[gt], writes=[gt])
        P.op("dve", lambda e: e.tensor_scalar(out=gt[:], in0=gt[:], scalar1=1.0, scalar2=None, op0=ALU.add), reads=[gt], writes=[gt])
        P.op("dve", lambda e: e.reciprocal(out=gt[:], in_=gt[:]), reads=[gt], writes=[gt])
        proj(ps_b, 512, wq, ZN)
        P.op("act", lambda e: e.activation(out=ez[:, 0:512], in_=ps_b[:, 0:512], func=AF.Exp, scale=-1.0), reads=[ps_b], writes=[ez])
        P.op("dve", lambda e: e.tensor_scalar(out=ez[:, 0:512], in0=ez[:, 0:512], scalar1=1.0, scalar2=None, op0=ALU.add),
             reads=[ez], writes=[ez])
        P.op("dve", lambda e: e.reciprocal(out=ez[:, 0:512], in_=ez[:, 0:512]), reads=[ez], writes=[ez])
        P.op("dve", lambda e: e.tensor_tensor(out=szn[:], in0=ps_b[:, 0:512], in1=ez[:, 0:512], op=ALU.mult),
             reads=[ps_b, ez], writes=[szn])
        if KP2 < 2:
            continue
        Nc = 64 * (r + 1)
        nch = (Nc + 127) // 128
        P.op("dve", lambda e: e.memset(rs[:], 0.0), writes=[rs])
        for h in range(8):
            g, hh = h // 4, h % 4
            for p0 in range(0, Nc, 512):
                p1 = min(Nc, p0 + 512)
                P.op("pe", lambda e, g=g, hh=hh, p0=p0, p1=p1: e.matmul(ps_c[:, p0:p1], lhsT=qT[g * 64:(g + 1) * 64, hh, :],
                                                                       rhs=kcT[g * 64:(g + 1) * 64, p0:p1], start=True, stop=True),
                     reads=[qT, kcT], writes=[ps_c0, ps_c1], nowaw=(p0 > 0))
            lo = max(0, Nc - 72)
            wd = Nc - lo
            P.op("dve", lambda e, lo=lo, wd=wd, Nc=Nc, par=par: e.tensor_tensor(out=ps_c[:, lo:Nc], in0=ps_c[:, lo:Nc], in1=cb[:, par, 72 - wd:72],
                                                                      op=ALU.add), reads=[ps_c0, ps_c1, cb], writes=[ps_c0, ps_c1])
            P.op("act", lambda e, h=h, Nc=Nc: e.activation(out=Pc[:, 0:Nc], in_=ps_c[:, 0:Nc], func=AF.Exp, accum_out=rs[:, h:h + 1]),
                 reads=[ps_c0, ps_c1], writes=[Pc, rs])
            P.op("dve", lambda e, h=h: e.tensor_scalar(out=rinv[:, h:h + 1], in0=rs[:, h:h + 1], scalar1=1e-30, scalar2=None, op0=ALU.max),
                 reads=[rs], writes=[rinv])
            P.op("dve", lambda e, h=h: e.reciprocal(out=rinv[:, h:h + 1], in_=rinv[:, h:h + 1]), reads=[rinv], writes=[rinv])
            if hh == 0:
                P.op("dve", lambda e, h=h, g=g, Nc=Nc: e.tensor_scalar(out=pg[g][:, 0:Nc], in0=Pc[:, 0:Nc], scalar1=rinv[:, h:h + 1],
                                                                        scalar2=None, op0=ALU.mult), reads=[Pc, rinv], writes=[pg[g]])
            else:
                P.op("dve", lambda e, h=h, g=g, Nc=Nc: e.scalar_tensor_tensor(out=pg[g][:, 0:Nc], in0=Pc[:, 0:Nc], scalar=rinv[:, h:h + 1],
                                                                               in1=pg[g][:, 0:Nc], op0=ALU.mult, op1=ALU.add),
                     reads=[Pc, rinv, pg[g]], writes=[pg[g]])
            for j in range(nch):
                P.op("pe", lambda e, j=j: e.transpose(out=ps_t[:, j * 128:(j + 1) * 128], in_=Pc[:, j * 128:(j + 1) * 128], identity=identb[:]),
                     reads=[Pc, identb], writes=[ps_t], nowaw=(j > 0))
            P.op("act", lambda e, nch=nch: e.activation(out=PcT[:, 0:nch * 128], in_=ps_t[:, 0:nch * 128], func=AF.Copy),
                 reads=[ps_t], writes=[PcT])
            for j in range(nch):
                P.op("pe", lambda e, j=j, h=h, g=g, nch=nch: e.matmul(ps_p[:, h * 64:(h + 1) * 64], lhsT=PcT[:, j * 128:(j + 1) * 128],
                                                                     rhs=vcS[:, j, g * 64:(g + 1) * 64], start=(j == 0), stop=(j == nch - 1)),
                     reads=[PcT, vcS], writes=[ps_p], nowaw=not (h == 0 and j == 0))
        P.op("dve", lambda e: e.tensor_tensor(out=sc8[:], in0=rinv[:], in1=gate3[:, :, 0], op=ALU.mult), reads=[rinv, gt], writes=[sc8])
        P.op("dve", lambda e: e.tensor_tensor(out=oa[:], in0=ps_p[:, 0:512].rearrange("p (h d) -> p h d", h=8),
                                              in1=sc8[:].unsqueeze(2).to_broadcast([128, 8, 64]), op=ALU.mult),
             reads=[ps_p, sc8], writes=[oa])
        if KP2 < 3:
            continue
        P.op("dve", lambda e, r=r: e.tensor_scalar(out=m1[:], in0=iota[:], scalar1=cur[:, r, 0:1], scalar2=None, op0=ALU.is_lt),
             reads=[iota, cur], writes=[m1])
        P.op("dve", lambda e: e.memset(m1[:, 0:1], 0.0), writes=[m1])
        P.op("dve", lambda e, r=r: e.tensor_scalar(out=m2[:], in0=iota[:], scalar1=cur[:, r, 0:1], scalar2=11.0, op0=ALU.is_ge, op1=ALU.mult),
             reads=[iota, cur], writes=[m2])
        P.op("dve", lambda e, r=r: e.tensor_scalar(out=ta[:], in0=iota[:], scalar1=cur[:, r, 1:2], scalar2=1.0, op0=ALU.is_ge, op1=ALU.mult),
             reads=[iota, cur], writes=[ta])
        P.op("dve", lambda e: e.tensor_tensor(out=m2[:], in0=m2[:], in1=ta[:], op=ALU.add), reads=[m2, ta], writes=[m2])
        P.op("dve", lambda e, r=r: e.tensor_scalar(out=ta[:], in0=iota[:], scalar1=cur[:, r, 1:2], scalar2=-13.0, op0=ALU.is_gt, op1=ALU.mult),
             reads=[iota, cur], writes=[ta])
        P.op("dve", lambda e: e.tensor_tensor(out=m2[:], in0=m2[:], in1=ta[:], op=ALU.add), reads=[m2, ta], writes=[m2])
        P.op("dve", lambda e: e.memset(m2[:, 0:1], 13.0), writes=[m2])
        KTK = int(os.environ.get("KTK", "9"))
        for g in range(2 if KTK >= 2 else 0):
            pg3 = pg[g][:].rearrange("p (n f) -> p n f", f=4)
            P.op("dve", lambda e, pg3=pg3: e.tensor_reduce(out=s4[:], in_=pg3, axis=AX.X, op=ALU.add), reads=[pg[g]], writes=[s4])
            P.op("dve", lambda e, pg3=pg3: e.tensor_tensor(out=s4[:, 1:256], in0=s4[:, 1:256], in1=pg3[:, 0:255, 3], op=ALU.add),
                 reads=[s4, pg[g]], writes=[s4])
            P.op("dve", lambda e: e.tensor_tensor(out=score[:], in0=s4[:], in1=m1[:], op=ALU.mult), reads=[s4, m1], writes=[score])
            P.op("dve", lambda e: e.tensor_tensor(out=score[:], in0=score[:], in1=m2[:], op=ALU.add), reads=[score, m2], writes=[score])
            P.op("dve", lambda e: e.max(out=mx[:, 0:8], in_=score[:]), reads=[score], writes=[mx])
            P.op("dve", lambda e: e.match_replace(out=rep[:], in_to_replace=mx[:, 0:8], in_values=score[:], imm_value=-5.0),
                 reads=[score, mx], writes=[rep])
            P.op("dve", lambda e: e.max(out=mx[:, 8:16], in_=rep[:]), reads=[rep], writes=[mx])
            P.op("dve", lambda e: e.tensor_scalar(out=thr[:], in0=mx[:, 15:16], scalar1=0.0, scalar2=None, op0=ALU.max),
                 reads=[mx], writes=[thr])
            P.op("dve", lambda e: e.tensor_scalar(out=biastm[:], in0=score[:], scalar1=thr[:], scalar2=NEG, op0=ALU.is_lt, op1=ALU.mult),
                 reads=[score, thr], writes=[biastm])
            if KTK < 3:
                continue
            for j in range(2):
                P.op("pe", lambda e, j=j: e.transpose(out=ps_t[:, j * 128:(j + 1) * 128], in_=biastm[:, j * 128:(j + 1) * 128], identity=identb[:]),
                     reads=[biastm, identb], writes=[ps_t], nowaw=(j > 0))
            for t in range(2):
                P.op("dve", lambda e, g=g, t=t: e.tensor_copy(out=BT8[g][t][t * 64:(t + 1) * 64, :, :],
                                                              in_=ps_t[t * 64:(t + 1) * 64, 0:256].rearrange("p (j q) -> p j q", j=2)),
                     reads=[ps_t], writes=[BT8[g][t]])
        if KP2 < 4:
            continue
        chunks = []
        for kk in range(8 * r + 8):
            bf_ = None
            if kk >= 8 * r:
                bf_ = (identb, identb[:], dbm, dbm[:, par, kk - 8 * r, :].unsqueeze(1).to_broadcast([128, 4, 128]))
            chunks.append((kk, bf_))
        attn_branch(r, chunks, 256, 0, 1)
        if KP2 < 5:
            continue
        chunks = []
        for i in range(12):
            kk = 8 * r - 4 + i
            if kk < 0:
                continue
            chunks.append((kk, (identb, identb[:], wb, wb[:, par, i, :].unsqueeze(1).to_broadcast([128, 4, 128]))))
        attn_branch(r, chunks, 384, 132, 2)
        P.op("dve", lambda e: e.tensor_tensor(out=A1[:], in0=oa[:].rearrange("p h d -> p (h d)"), in1=szn[:], op=ALU.mult),
             reads=[oa, szn], writes=[A1])
        P.dma("pool", lab[r * 128:(r + 1) * 128, 0:512], A1[:], reads=[A1], writes=[lab], nowaw=True)
        if KP2 < 6:
            continue
        proj(ps_a, 512, wq, QR)
        P.op("act", lambda e: e.activation(out=pf2a[:, 0:512], in_=ps_a[:, 0:512], func=AF.Copy, scale=float(128 ** -0.5)),
             reads=[ps_a], writes=[pf2a])
        rotary((pf2a, pf2a[:, 0:512].rearrange("p (h d) -> p h d", h=4)), (qr, qr[:].rearrange("p (h d) -> p h d", h=4)),
               4, 128, 64, cosR[:, r, :], sinR[:, r, :])
        P.op("dve", lambda e: e.tensor_tensor(out=qd[:].rearrange("p (h d) -> p h d", h=4), in0=qr[:].rearrange("p (h d) -> p h d", h=4),
                                              in1=qdec[:].unsqueeze(2).to_broadcast([128, 4, 128]), op=ALU.mult),
             reads=[qr, qdec], writes=[qd])
        P.dma("sp", krvr2[:], lkv[r * 128:(r + 1) * 128, :], reads=[lkv], writes=[krvr2])
        for h in range(4):
            P.op("pe", lambda e, h=h: e.transpose(out=ps_t[:, h * 128:(h + 1) * 128], in_=qr[:, h * 128:(h + 1) * 128], identity=identb[:]),
                 reads=[qr, identb], writes=[ps_t], nowaw=(h > 0))
        for h in range(4):
            P.op("pe", lambda e, h=h: e.transpose(out=ps_t[:, (4 + h) * 128:(5 + h) * 128], in_=krvr2[:, h * 128:(h + 1) * 128], identity=identb[:]),
                 reads=[krvr2, identb], writes=[ps_t], nowaw=True)
        P.op("dve", lambda e: e.tensor_copy(out=qkT[:], in_=ps_t[:].rearrange("p (h t) -> p h t", h=8)),
             reads=[ps_t], writes=[qkT])
        for h in range(4):
            P.op("pe", lambda e, h=h: e.transpose(out=ps_t[:, h * 128:(h + 1) * 128], in_=qd[:, h * 128:(h + 1) * 128], identity=identb[:]),
                 reads=[qd, identb], writes=[ps_t], nowaw=(h > 0))
        P.op("dve", lambda e: e.tensor_copy(out=qdT[:], in_=ps_t[:, 0:512].rearrange("p (h t) -> p h t", h=4)),
             reads=[ps_t], writes=[qdT])
        for h in range(4):
            P.op("pe", lambda e, h=h: e.matmul(ps_b[:, h * 128:(h + 1) * 128], lhsT=qkT[:, 4 + h, :], rhs=qkT[:, h, :], start=True, stop=True),
                 reads=[qkT], writes=[ps_b], nowaw=(h > 0))
        P.op("dve", lambda e: e.tensor_tensor(out=attT[:], in0=ps_b[:, 0:512].rearrange("p (h t) -> p h t", h=4), in1=dmask[:], op=ALU.mult),
             reads=[ps_b, dmask], writes=[attT])
        for h in range(4):
            P.op("pe", lambda e, h=h: e.matmul(ps_c[:, h * 256:(h + 1) * 256], lhsT=attT[:, h, :], rhs=krvr2[:, 512 + h * 256:512 + (h + 1) * 256],
                                               start=True, stop=False), reads=[attT, krvr2], writes=[ps_c0, ps_c1], nowaw=(h > 0))
            P.op("pe", lambda e, h=h: e.matmul(ps_c[:, h * 256:(h + 1) * 256], lhsT=qdT[:, h, :], rhs=Rb[:, h * 256:(h + 1) * 256],
                                               start=False, stop=True), reads=[qdT, Rb], writes=[ps_c0, ps_c1], nowaw=True)
        P.op("act", lambda e: e.activation(out=ob[:], in_=ps_c[:], func=AF.Copy), reads=[ps_c0, ps_c1], writes=[ob])
        ob3 = ob[:].rearrange("p (h e) -> p h e", h=4)
        P.op("dve", lambda e: e.tensor_reduce(out=sm4[:], in_=ob3, axis=AX.X, op=ALU.add), reads=[ob], writes=[sm4])
        P.op("dve", lambda e: e.tensor_scalar(out=sm4[:], in0=sm4[:], scalar1=1.0 / 256, scalar2=None, op0=ALU.mult), reads=[sm4], writes=[sm4])
        P.op("dve", lambda e: e.tensor_tensor(out=ob3, in0=ob3, in1=sm4[:].unsqueeze(2).to_broadcast([128, 4, 256]), op=ALU.subtract),
             reads=[ob, sm4], writes=[ob])
        P.op("pool", lambda e: e.tensor_tensor(out=tmpkv[:], in0=ob[:], in1=ob[:], op=ALU.mult), reads=[ob], writes=[tmpkv])
        P.op("dve", lambda e: e.tensor_reduce(out=sm4[:], in_=tmpkv[:].rearrange("p (h e) -> p h e", h=4), axis=AX.X, op=ALU.add),
             reads=[tmpkv], writes=[sm4])
        P.op("dve", lambda e: e.tensor_scalar(out=sm4[:], in0=sm4[:], scalar1=1.0 / 256, scalar2=1e-6, op0=ALU.mult, op1=ALU.add),
             reads=[sm4], writes=[sm4])
        P.op("act", lambda e: e.activation(out=sm4[:], in_=sm4[:], func=AF.Ln), reads=[sm4], writes=[sm4])
        P.op("act", lambda e: e.activation(out=sm4[:], in_=sm4[:], func=AF.Exp, scale=-0.5), reads=[sm4], writes=[sm4])
        P.op("dve", lambda e: e.tensor_tensor(out=ob3, in0=ob3, in1=sm4[:].unsqueeze(2).to_broadcast([128, 4, 256]), op=ALU.mult),
             reads=[ob, sm4], writes=[ob])
        proj(ps_a, 512, wq, ZR)
        proj(ps_b, 512, wq, ZR + 512)
        P.op("act", lambda e: e.activation(out=ez[:, 0:512], in_=ps_a[:, 0:512], func=AF.Exp, scale=-1.0), reads=[ps_a], writes=[ez])
        P.op("act", lambda e: e.activation(out=ez[:, 512:1024], in_=ps_b[:, 0:512], func=AF.Exp, scale=-1.0), reads=[ps_b], writes=[ez], nowaw=True)
        P.op("dve", lambda e: e.tensor_scalar(out=ez[:], in0=ez[:], scalar1=1.0, scalar2=None, op0=ALU.add), reads=[ez], writes=[ez])
        P.op("dve", lambda e: e.reciprocal(out=ez[:], in_=ez[:]), reads=[ez], writes=[ez])
        P.op("dve", lambda e: e.tensor_tensor(out=ez[:, 0:512], in0=ps_a[:, 0:512], in1=ez[:, 0:512], op=ALU.mult), reads=[ps_a, ez], writes=[ez])
        P.op("dve", lambda e: e.tensor_tensor(out=ez[:, 512:1024], in0=ps_b[:, 0:512], in1=ez[:, 512:1024], op=ALU.mult),
             reads=[ps_b, ez], writes=[ez])
        P.op("dve", lambda e: e.tensor_tensor(out=B1[:], in0=ob[:], in1=ez[:], op=ALU.mult), reads=[ob, ez], writes=[B1])
        P.dma("pool", lab[r * 128:(r + 1) * 128, 512:1536], B1[:], reads=[B1], writes=[lab], nowaw=True)
    P.barrier()
    es3.close()
    if debug:
        P.dma("sp", dbg["lab"][:], lab[:], reads=[lab], writes=[dbg["lab"]])

    es4 = ExitStack()
    P4 = Prog.__new__(Prog)
    P4.__dict__ = P.__dict__.copy()
    P4.es = es4
    xt4 = P4.sb("xt4", [128, D], F32)
    W.update(junk=P4.sb("junk4", [128, D], BF16), ss=P4.sb("ss4", [128, 1], F32), rstd=P4.sb("rstd4", [128, 1], F32),
             xn=P4.sb("xn4", [128, D], BF16), hT=P4.sb("hT4", [128, 8, 128], BF16), rt=None)
    hT = W["hT"]
    wm = P4.sb("wm", [128, 8, 2048], BF16)
    wno = P4.sb("wno", [128, 4, 1024], BF16)
    wro = P4.sb("wro", [128, 8, 1024], BF16)
    wo = P4.sb("wo", [128, 8, 1024], BF16)
    npost = P4.sb("npost", [128, D], F32)
    stg4 = [P4.sb("wstg4_%d" % i, [128, 1024], F32) for i in range(2)]
    P.dma("sp", npost[:], npost_d.t.partition_broadcast(128), writes=[npost])
    ci = 0
    jobs = []
    for kc in range(8):
        jobs.append((w_in_d[kc * 128:(kc + 1) * 128, C_MA:C_MA + 1024], wm, (kc, 0)))
        jobs.append((w_in_d[kc * 128:(kc + 1) * 128, C_MB:C_MB + 1024], wm, (kc, 1024)))
        jobs.append((wro_d[kc * 128:(kc + 1) * 128, :], wro, (kc, 0)))
        jobs.append((wo_d[kc * 128:(kc + 1) * 128, :], wo, (kc, 0)))
        if kc < 4:
            jobs.append((wno_d[kc * 128:(kc + 1) * 128, :], wno, (kc, 0)))
    for src, dstb, (kc, o0) in jobs:
        st = stg4[ci % 2]
        P.dma("sp", st[:], src, writes=[st])
        en = cast_eng[ci % 3]
        if en == "act":
            P.op("act", lambda e, st=st, dstb=dstb, kc=kc, o0=o0: e.activation(out=dstb[:, kc, o0:o0 + 1024], in_=st[:], func=AF.Copy),
                 reads=[st], writes=[dstb], nowaw=True)
        else:
            P.op(en, lambda e, st=st, dstb=dstb, kc=kc, o0=o0: e.tensor_copy(out=dstb[:, kc, o0:o0 + 1024], in_=st[:]),
                 reads=[st], writes=[dstb], nowaw=True)
        ci += 1
    AB = P4.sb("AB", [128, 1536], BF16)
    ABT = P4.sb("ABT", [128, 12, 128], BF16)
    sg = P4.sb("sg", [128, 512], F32)
    mg = P4.sb("mg", [128, 1024], F32)
    tq = P4.sb("tq", [128, 512], F32)
    mgb = P4.sb("mgb", [128, 1024], BF16)
    mT = P4.sb("mT", [128, 8, 128], BF16)
    yo = P4.sb("yo", [128, D], F32)
    ss5 = P4.sb("ss5", [128, 1], F32)
    junk5 = P4.sb("junk5", [128, D], BF16)
    KP3 = int(os.environ.get("KP3", "99"))
    for r in range(NR if KP3 >= 1 else 0):
        P.dma("sp", xt4[:], x_d[r], writes=[xt4])
        rms_front(xt4, gcol, hT)
        P.dma("sp", AB[:], lab[r * 128:(r + 1) * 128, :], reads=[lab], writes=[AB])
        for j in range(8):
            P.op("pe", lambda e, j=j: e.transpose(out=ps_t[:, j * 128:(j + 1) * 128], in_=AB[:, j * 128:(j + 1) * 128], identity=identb[:]),
                 reads=[AB, identb], writes=[ps_t], nowaw=(j > 0))
        P.op("dve", lambda e: e.tensor_copy(out=ABT[:, 0:8, :], in_=ps_t[:].rearrange("p (h t) -> p h t", h=8)),
             reads=[ps_t], writes=[ABT])
        for j in range(4):
            P.op("pe", lambda e, j=j: e.transpose(out=ps_t[:, j * 128:(j + 1) * 128], in_=AB[:, (8 + j) * 128:(9 + j) * 128], identity=identb[:]),
                 reads=[AB, identb], writes=[ps_t], nowaw=(j > 0))
        P.op("dve", lambda e: e.tensor_copy(out=ABT[:, 8:12, :], in_=ps_t[:, 0:512].rearrange("p (h t) -> p h t", h=4)),
             reads=[ps_t], writes=[ABT], nowaw=True)
        for half in range(2):
            hs = slice(half * 512, (half + 1) * 512)
            for j in range(4):
                P.op("pe", lambda e, j=j, hs=hs: e.matmul(ps_a[:, 0:512], lhsT=ABT[:, j, :], rhs=wno[:, j, hs], start=(j == 0), stop=(j == 3)),
                     reads=[ABT, wno], writes=[ps_a], nowaw=(j > 0))
            proj(ps_b, 512, wm, half * 512)
            P.op("act", lambda e: e.activation(out=sg[:], in_=ps_b[:, 0:512], func=AF.Exp, scale=-1.0), reads=[ps_b], writes=[sg])
            P.op("dve", lambda e: e.tensor_scalar(out=sg[:], in0=sg[:], scalar1=1.0, scalar2=None, op0=ALU.add), reads=[sg], writes=[sg])
            P.op("dve", lambda e: e.reciprocal(out=sg[:], in_=sg[:]), reads=[sg], writes=[sg])
            P.op("dve", lambda e, hs=hs: e.tensor_tensor(out=mg[:, hs], in0=ps_a[:, 0:512], in1=sg[:], op=ALU.mult),
                 reads=[ps_a, sg], writes=[mg], nowaw=(half > 0))
            for j in range(8):
                P.op("pe", lambda e, j=j, hs=hs: e.matmul(ps_d[:, 0:512], lhsT=ABT[:, 4 + j, :], rhs=wro[:, j, hs], start=(j == 0), stop=(j == 7)),
                     reads=[ABT, wro], writes=[ps_d], nowaw=(j > 0))
            proj(ps_e, 512, wm, 1024 + half * 512)
            P.op("act", lambda e: e.activation(out=sg[:], in_=ps_e[:, 0:512], func=AF.Exp, scale=-1.0), reads=[ps_e], writes=[sg])
            P.op("dve", lambda e: e.tensor_scalar(out=sg[:], in0=sg[:], scalar1=1.0, scalar2=None, op0=ALU.add), reads=[sg], writes=[sg])
            P.op("dve", lambda e: e.reciprocal(out=sg[:], in_=sg[:]), reads=[sg], writes=[sg])
            P.op("dve", lambda e: e.tensor_tensor(out=tq[:], in0=ps_d[:, 0:512], in1=sg[:], op=ALU.mult), reads=[ps_d, sg], writes=[tq])
            P.op("dve", lambda e, hs=hs: e.tensor_tensor(out=mg[:, hs], in0=mg[:, hs], in1=tq[:], op=ALU.add), reads=[mg, tq], writes=[mg])
        P.op("dve", lambda e: e.tensor_copy(out=mgb[:], in_=mg[:]), reads=[mg], writes=[mgb])
        for j in range(8):
            P.op("pe", lambda e, j=j: e.transpose(out=ps_t[:, j * 128:(j + 1) * 128], in_=mgb[:, j * 128:(j + 1) * 128], identity=identb[:]),
                 reads=[mgb, identb], writes=[ps_t], nowaw=(j > 0))
        P.op("dve", lambda e: e.tensor_copy(out=mT[:], in_=ps_t[:].rearrange("p (h t) -> p h t", h=8)), reads=[ps_t], writes=[mT])
        for half in range(2):
            hs = slice(half * 512, (half + 1) * 512)
            for k in range(8):
                P.op("pe", lambda e, k=k, hs=hs: e.matmul(ps_c[:, hs], lhsT=mT[:, k, :], rhs=wo[:, k, hs], start=(k == 0), stop=(k == 7)),
                     reads=[mT, wo], writes=[ps_c0, ps_c1], nowaw=not (half == 0 and k == 0))
        P.op("act", lambda e: e.activation(out=junk5[:], in_=ps_c[:], func=AF.Square, accum_out=ss5[:]), reads=[ps_c0, ps_c1], writes=[junk5, ss5])
        P.op("dve", lambda e: e.tensor_scalar(out=ss5[:], in0=ss5[:], scalar1=1.0 / D, scalar2=1e-6, op0=ALU.mult, op1=ALU.add),
             reads=[ss5], writes=[ss5])
        P.op("act", lambda e: e.activation(out=ss5[:], in_=ss5[:], func=AF.Ln), reads=[ss5], writes=[ss5])
        P.op("act", lambda e: e.activation(out=ss5[:], in_=ss5[:], func=AF.Exp, scale=-0.5), reads=[ss5], writes=[ss5])
        P.op("dve", lambda e: e.scalar_tensor_tensor(out=yo[:], in0=ps_c[:], scalar=ss5[:], in1=npost[:], op0=ALU.mult, op1=ALU.mult),
             reads=[ps_c0, ps_c1, ss5, npost], writes=[yo])
        P.op("dve", lambda e: e.tensor_tensor(out=yo[:], in0=yo[:], in1=xt4[:], op=ALU.add), reads=[yo, xt4], writes=[yo])
        P.dma("pool", y_d[r], yo[:], reads=[yo], writes=[y_d], nowaw=True)
    P.barrier()
    es4.close()

    P.emit()
    es.close()
    return nc


_CACHE = {}


def _consts():
    identb = np.eye(128, dtype=np.float32).astype(ml_dtypes.bfloat16)
    identf = np.eye(128, dtype=np.float32)
    invn = (1.0 / (np.float32(500000.0) ** (np.arange(0, 16, 2, dtype=np.float32) / np.float32(16)))).astype(np.float32)
    invr = (1.0 / (np.float32(10000.0) ** np.linspace(0.0, 1.0, 64, dtype=np.float32))).astype(np.float32)
    lg = np.log(1.0 - 2.0 ** (-5.0 - np.arange(4, dtype=np.float64)))
    n = np.arange(128, dtype=np.float64)
    kdec = np.exp((127.0 - n)[:, None] * lg[None, :]).astype(np.float32)
    gam = 1.0 - 2.0 ** (-5.0 - np.arange(4, dtype=np.float64))
    qdec = (gam[None, :] ** (n[:, None] + 1.0)).astype(np.float32)
    dif = n[None, :] - n[:, None]
    dmask = np.where(dif[:, None, :] >= 0, gam[None, :, None] ** np.maximum(dif[:, None, :], 0.0), 0.0).astype(np.float32)
    e32 = np.zeros((128, 32, 128), dtype=np.float32)
    for kkl in range(32):
        for key in range(128):
            e32[2 * kkl + key // 64, kkl, key] = 1.0
            e32[64 + 2 * kkl + key // 64, kkl, key] = 1.0
    iota = np.broadcast_to(np.arange(256, dtype=np.float32)[None, :], (128, 256)).copy()
    return dict(identb=identb, identf=identf, kdec=kdec, qdec=qdec, dmask=dmask,
                e32=e32.astype(ml_dtypes.bfloat16), iota=iota), invn, invr


def _core_tables(c):
    gam = 1.0 - 2.0 ** (-5.0 - np.arange(4, dtype=np.float64))
    key = np.arange(128)[:, None]
    q = np.arange(128)[None, :]
    wb = np.zeros((128, 2, 12, 128), dtype=np.float32)
    db = np.zeros((128, 2, 8, 128), dtype=np.float32)
    cb = np.zeros((128, 2, 72), dtype=np.float32)
    for par in range(2):
        j = c if par == 0 else 7 - c
        for i in range(12):
            diff = 128 * (j + 4 - i) + q - key
            wb[:, par, i, :] = np.where((diff >= 0) & (diff < 512), 0.0, NEG)
        for i in range(8):
            if i < j:
                db[:, par, i, :] = 0.0
            elif i == j:
                db[:, par, i, :] = np.where(key <= q, 0.0, NEG)
            else:
                db[:, par, i, :] = NEG
        ii = np.arange(72)[None, :]
        qq = np.arange(128)[:, None]
        cb[:, par, :] = np.where(16 * ii - 97 <= 128 * j + qq, 0.0, NEG)
    tiles = [tile_of(c, r) for r in range(NR)]
    cur = np.zeros((128, NR, 2), dtype=np.float32)
    for r in range(NR):
        cv = 2 * tiles[r] + (np.arange(128) >= 64)
        cur[:, r, 0] = cv - 1
        cur[:, r, 1] = cv
    wret = np.zeros((NR, 16, 4), dtype=np.float64)
    dfac = np.ones((NR, 4), dtype=np.float64)
    for r in range(NR):
        b = tiles[r]
        bp = tiles[r - 1] if r > 0 else 0
        if r > 0:
            dfac[r] = gam ** (128.0 * (b - bp))
        for mi in range(16):
            m = 8 * (r - 1) + mi
            if m >= bp and m < b and m >= 0:
                wret[r, mi] = gam ** (128.0 * (b - 1 - m))
    return dict(wb=wb.astype(ml_dtypes.bfloat16), db=db.astype(ml_dtypes.bfloat16), cb=cb, cur=cur,
                wret=np.broadcast_to(wret.astype(np.float32)[None], (128, NR, 16, 4)).copy(),
                dfac=np.broadcast_to(dfac.astype(np.float32)[None], (128, NR, 4)).copy())


def kernel(x, norm_pre, w_in, b_nsa_gate, cmp_pe_k, cmp_w1_k, cmp_w2_k,
           cmp_pe_v, cmp_w1_v, cmp_w2_v, w_nsa_o, w_ret_o, w_out, norm_post, _debug=False):
    x = np.asarray(x, dtype=np.float32).reshape(S, D)
    xt = x.reshape(128, 128, D)
    key = "dbg" if _debug else "nc"
    if key not in _CACHE:
        _CACHE[key] = build(debug=_debug)
    nc = _CACHE[key]
    cs, invn, invr = _consts()
    slot_tiles = [tile_of(sl // NR, sl % NR) for sl in range(128)]
    x_all = np.ascontiguousarray(xt[slot_tiles])
    pos_all = (np.array(slot_tiles, dtype=np.float32)[None, :] * 128 + np.arange(128, dtype=np.float32)[:, None])
    angn_a = (pos_all[:, :, None] * invn[None, None, :]).astype(np.float32)
    angr_a = (pos_all[:, :, None] * invr[None, None, :]).astype(np.float32)
    cs["x_all"] = x_all
    cs["cosn_all"] = np.cos(angn_a).astype(np.float32)
    cs["sinn_all"] = np.sin(angn_a).astype(np.float32)
    cs["cosr_all"] = np.ascontiguousarray(np.cos(angr_a).astype(np.float32).transpose(1, 0, 2))
    cs["sinr_all"] = np.ascontiguousarray(np.sin(angr_a).astype(np.float32).transpose(1, 0, 2))
    in_maps = []
    for c in range(NCORES):
        tiles = [tile_of(c, r) for r in range(NR)]
        m = dict(cs)
        m["x"] = np.ascontiguousarray(xt[tiles])
        posc = (np.array(tiles, dtype=np.float32)[None, :] * 128 + np.arange(128, dtype=np.float32)[:, None])
        angn = (posc[:, :, None] * invn[None, None, :]).astype(np.float32)
        angr = (posc[:, :, None] * invr[None, None, :]).astype(np.float32)
        m["cosn"] = np.cos(angn).astype(np.float32)
        m["sinn"] = np.sin(angn).astype(np.float32)
        m["cosr"] = np.cos(angr).astype(np.float32)
        m["sinr"] = np.sin(angr).astype(np.float32)
        m["w_in"] = np.asarray(w_in, dtype=np.float32)
        m["norm_pre"] = np.asarray(norm_pre, dtype=np.float32)
        m.update(_core_tables(c))
        for nm, arr in (("cmp_w1_k", cmp_w1_k), ("cmp_w2_k", cmp_w2_k), ("cmp_pe_k", cmp_pe_k),
                        ("cmp_w1_v", cmp_w1_v), ("cmp_w2_v", cmp_w2_v), ("cmp_pe_v", cmp_pe_v),
                        ("b_nsa_gate", b_nsa_gate), ("norm_post", norm_post), ("w_nsa_o", w_nsa_o),
                        ("w_ret_o", w_ret_o), ("w_out", w_out)):
            m[nm] = np.asarray(arr, dtype=np.float32)
        in_maps.append(m)
    res = run_bass_kernel_spmd(nc, in_maps, core_ids=list(range(NCORES)))
    out = np.zeros((128, 128, D), dtype=np.float32)
    for c in range(NCORES):
        y = res.results[c]["y"]
        for r in range(NR):
            out[tile_of(c, r)] = y[r]
    if _debug:
        return out.reshape(1, S, D), res
    return out.reshape(1, S, D)
```

```python
import os
from contextlib import ExitStack
import numpy as np
import ml_dtypes
import concourse.bass as bass
import concourse.mybir as mybir
from concourse.bass_utils import run_bass_kernel_spmd

F32 = mybir.dt.float32
BF16 = mybir.dt.bfloat16
ALU = mybir.AluOpType
AF = mybir.ActivationFunctionType
AX = mybir.AxisListType

NCORES = 8
NR = 16
S = 16384
D = 1024
PT = 6936
NEG = -30000.0
ND = 8

C_QN, C_KC, C_VC, C_KS, C_VS, C_KW, C_VW, C_GN, C_ZN, C_QR, C_KR, C_VR, C_ZR, C_MA, C_MB = (
    0, 512, 640, 768, 896, 1024, 1152, 1280, 1304, 1816, 2328, 2840, 3864, 4888, 5912)


def tile_of(c, r):
    return 8 * r + (c if r % 2 == 0 else 7 - c)


def slot_of(b):
    r = b // 8
    j = b % 8
    c = j if r % 2 == 0 else 7 - j
    return c * NR + r


class Buf:
    __slots__ = ("t", "w", "r", "excl")

    def __init__(self, t, excl=False):
        self.t = t
        self.w = {}
        self.r = {}
        self.excl = excl

    def __getitem__(self, idx):
        return self.t[idx]


class Stream:
    def __init__(self, name, selfsync):
        self.name = name
        self.selfsync = selfsync
        self.count = 0
        self.ndma = 0
        self.seen = {}
        self.ops = []


class Prog:
    def __init__(self, nc, es):
        self.nc = nc
        self.es = es
        self.sems = {}
        self.streams = {}
        for name, ss in (("pe", False), ("act", True), ("dve", True), ("pool", True), ("sp", False)):
            self.streams[name] = Stream(name, ss)
            self.sems[name] = es.enter_context(nc.semaphore("s_" + name))
        for q in ("sp", "pool"):
            for i in range(ND):
                k = "%s_d%d" % (q, i)
                self.sems[k] = es.enter_context(nc.semaphore(k))
        self.ncc = 0

    def sb(self, name, shape, dt):
        return Buf(self.es.enter_context(self.nc.sbuf_tensor("sb_" + name, list(shape), dt)))

    def ps(self, name, shape, dt):
        return Buf(self.es.enter_context(self.nc.psum_tensor(name, list(shape), dt)), excl=True)

    def _waits(self, s, reads, writes, nowaw):
        need = {}
        for b in reads:
            for k, v in b.w.items():
                if need.get(k, 0) < v:
                    need[k] = v
        for b in writes:
            for k, v in b.r.items():
                if need.get(k, 0) < v:
                    need[k] = v
            if not nowaw:
                for k, v in b.w.items():
                    if need.get(k, 0) < v:
                        need[k] = v
        waits = []
        for k, v in need.items():
            if k == s.name and not s.selfsync:
                continue
            if s.seen.get(k, 0) >= v:
                continue
            s.seen[k] = v
            waits.append((k, v))
        return waits

    def _book(self, ev, reads, writes, nowaw):
        k, v = ev
        for b in reads:
            b.r[k] = v
        for b in writes:
            if nowaw:
                b.w[k] = v
            else:
                b.w = {k: v}
                b.r = {}

    def op(self, sname, fn, reads=(), writes=(), nowaw=False):
        s = self.streams[sname]
        if sname != "pe":
            ex = [b for b in reads if b.excl]
            if ex:
                reads = [b for b in reads if not b.excl]
                writes = list(writes) + ex
                nowaw = False
        waits = self._waits(s, reads, writes, nowaw)
        s.count += 1
        s.ops.append((waits, fn, (s.name, 1)))
        self._book((s.name, s.count), reads, writes, nowaw)

    def dma(self, qname, out_ap, in_ap, reads=(), writes=(), nowaw=False, slow=False):
        q = self.streams[qname]
        i = q.ndma
        q.ndma += 1
        k = "%s_d%d" % (qname, i % ND)
        v = 16 * (i // ND + 1)
        waits = self._waits(q, reads, writes, nowaw)
        if v > 16 and q.seen.get(k, 0) < v - 16:
            waits.append((k, v - 16))
            q.seen[k] = v - 16
        if slow:
            fn = lambda e: e.dma_start(out=out_ap, in_=in_ap, allow_slow_non_contiguous=True)
        else:
            fn = lambda e: e.dma_start(out=out_ap, in_=in_ap)
        q.ops.append((waits, fn, (k, 16)))
        self._book((k, v), reads, writes, nowaw)

    def collective(self, in_buf, out_buf, in_ap, out_ap):
        s = self.streams["pool"]
        k = "cc%d" % self.ncc
        self.ncc += 1
        self.sems[k] = self.es.enter_context(self.nc.semaphore(k))
        waits = self._waits(s, [in_buf], [out_buf], False)

        def fn(e):
            return e.collective_compute("AllGather", ALU.bypass,
                                        replica_groups=[list(range(NCORES))],
                                        ins=[in_ap], outs=[out_ap])
        s.ops.append((waits, fn, (k, 1)))
        self._book((k, 1), [in_buf], [out_buf], False)

    def barrier(self):
        tot = {}
        for name, s in self.streams.items():
            if name in ("pe", "act", "dve", "pool"):
                tot[name] = s.count
            if name in ("sp", "pool"):
                for i in range(min(ND, s.ndma)):
                    n_i = (s.ndma - 1 - i) // ND + 1
                    tot["%s_d%d" % (name, i)] = 16 * n_i
        for i in range(self.ncc):
            tot["cc%d" % i] = 1
        for name, s in self.streams.items():
            waits = []
            for k, v in tot.items():
                if v > 0 and s.seen.get(k, 0) < v and not (k == name and name == "pe"):
                    waits.append((k, v))
                    s.seen[k] = v
            if waits:
                s.ops.append((waits, None, None))

    def emit(self):
        nc = self.nc
        self.barrier()
        sems = self.sems

        def run(s, e):
            for waits, fn, inc in s.ops:
                for k, v in waits:
                    e.wait_ge(sems[k], v)
                if fn is not None:
                    ins = fn(e)
                    ins.then_inc(sems[inc[0]], inc[1])

        with nc.Block() as block:
            @block.sync
            def _(e):
                run(self.streams["sp"], e)

            @block.tensor
            def _(e):
                run(self.streams["pe"], e)

            @block.scalar
            def _(e):
                run(self.streams["act"], e)

            @block.vector
            def _(e):
                run(self.streams["dve"], e)

            @block.gpsimd
            def _(e):
                run(self.streams["pool"], e)


def build(debug=False):
    nc = bass.Bass("TRN2", target_bir_lowering=False)
    es = ExitStack()
    P = Prog(nc, es)

    def din(name, shape, dt=F32):
        return Buf(nc.dram_tensor(name, list(shape), dt, kind="ExternalInput").ap())

    def dout(name, shape, dt=F32):
        return Buf(nc.dram_tensor(name, list(shape), dt, kind="ExternalOutput").ap())

    def dint(name, shape, dt):
        return Buf(nc.dram_tensor(name, list(shape), dt, kind="Internal").ap())

    x_d = din("x", [NR, 128, D])
    cosn_d = din("cosn", [128, NR, 8])
    sinn_d = din("sinn", [128, NR, 8])
    cosr_d = din("cosr", [128, NR, 64])
    sinr_d = din("sinr", [128, NR, 64])
    w_in_d = din("w_in", [D, PT])
    norm_pre_d = din("norm_pre", [D])
    identb_d = din("identb", [128, 128], BF16)
    identf_d = din("identf", [128, 128])
    kdec_d = din("kdec", [128, 4])
    y_d = dout("y", [NR, 128, D])
    xall_d = din("x_all", [128, 128, D])
    cosna_d = din("cosn_all", [128, 128, 8])
    sinna_d = din("sinn_all", [128, 128, 8])
    cosra_d = din("cosr_all", [128, 128, 64])
    sinra_d = din("sinr_all", [128, 128, 64])
    wb_d = din("wb", [128, 2, 12, 128], BF16)
    db_d = din("db", [128, 2, 8, 128], BF16)
    cb_d = din("cb", [128, 2, 72])
    e32_d = din("e32", [128, 32, 128], BF16)
    iota_d = din("iota", [128, 256])
    cur_d = din("cur", [128, NR, 2])
    qdec_d = din("qdec", [128, 4])
    dmask_d = din("dmask", [128, 4, 128])
    wret_d = din("wret", [128, NR, 16, 4])
    dfac_d = din("dfac", [128, NR, 4])
    bgate_d = din("b_nsa_gate", [24])
    npost_d = din("norm_post", [D])
    wno_d = din("w_nsa_o", [512, D])
    wro_d = din("w_ret_o", [D, D])
    wo_d = din("w_out", [D, D])
    lab = dint("lab", [NR * 128, 1536], BF16)
    w1k_d = din("cmp_w1_k", [2048, 256])
    w2k_d = din("cmp_w2_k", [256, 64])
    pek_d = din("cmp_pe_k", [32, 64])
    w1v_d = din("cmp_w1_v", [2048, 256])
    w2v_d = din("cmp_w2_v", [256, 64])
    pev_d = din("cmp_pe_v", [32, 64])

    g4_in = dint("g4_in", [NR * 128, 512], BF16)
    g4 = dint("g4", [NCORES * NR * 128, 512], BF16)
    gv_in = dint("gv_in", [NR * 128, 264], BF16)
    gv = dint("gv", [NCORES * NR * 128, 264], BF16)
    gkv_in = dint("gkv_in", [NR * 128, 1024], BF16)
    gkv = dint("gkv", [NCORES * NR * 128, 1024], BF16)
    lkv = dint("lkv", [NR * 128, 1536], BF16)

    dbg = {}
    if debug:
        dbg["g4"] = dout("dbg_g4", [NCORES * NR * 128, 512], BF16)
        dbg["gv"] = dout("dbg_gv", [NCORES * NR * 128, 264], BF16)
        dbg["gkv"] = dout("dbg_gkv", [NCORES * NR * 128, 1024], BF16)
        dbg["lkv"] = dout("dbg_lkv", [NR * 128, 1536], BF16)
        dbg["kcT"] = dout("dbg_kcT", [128, 1024], BF16)
        dbg["lab"] = dout("dbg_lab", [NR * 128, 1536], BF16)
        dbg["vcS"] = dout("dbg_vcS", [128, 8, 128], BF16)

    identb = P.sb("identb", [128, 128], BF16)
    identf = P.sb("identf", [128, 128], F32)
    gcol = P.sb("gcol", [128, 8], F32)
    kdec = P.sb("kdec", [128, 4], F32)
    cosN = P.sb("cosN", [128, NR, 8], F32)
    sinN = P.sb("sinN", [128, NR, 8], F32)
    cosR = P.sb("cosR", [128, NR, 64], F32)
    sinR = P.sb("sinR", [128, NR, 64], F32)

    ps_a = P.ps("ps_a", [128, 512], F32)
    ps_b = P.ps("ps_b", [128, 512], F32)
    ps_c = P.ps("ps_c", [128, 1024], F32)
    ps_c0 = Buf(ps_c.t, excl=True)
    ps_c1 = Buf(ps_c.t, excl=True)
    ps_d = P.ps("ps_d", [128, 512], F32)
    ps_e = P.ps("ps_e", [128, 512], F32)
    ps_p = P.ps("ps_p", [128, 512], F32)
    ps_t = P.ps("ps_t", [128, 1024], BF16)

    P.dma("sp", identb[:], identb_d[:], writes=[identb])
    P.dma("sp", identf[:], identf_d[:], writes=[identf])
    P.dma("sp", gcol[:], norm_pre_d.t.rearrange("(c p) -> p c", p=128), writes=[gcol], slow=True)
    P.dma("sp", kdec[:], kdec_d[:], writes=[kdec])

    P.dma("sp", cosN[:], cosn_d[:], writes=[cosN])
    P.dma("sp", sinN[:], sinn_d[:], writes=[sinN])
    P.dma("sp", cosR[:], cosr_d[:], writes=[cosR])
    P.dma("sp", sinR[:], sinr_d[:], writes=[sinR])

    es1 = ExitStack()
    P1 = Prog.__new__(Prog)
    P1.__dict__ = P.__dict__.copy()
    P1.es = es1
    NKV = 768 + 1536
    wkv = P1.sb("wkv", [128, 8, NKV], BF16)
    stg = [P1.sb("wstg%d" % i, [128, 1536], F32) for i in range(2)]
    cast_eng = ["dve", "pool", "act"]
    ci = 0
    for kc in range(8):
        for (c0, n, o0) in ((C_KC, 768, 0), (C_KR, 1536, 768)):
            st = stg[ci % 2]
            P.dma("sp", st[:, 0:n], w_in_d[kc * 128:(kc + 1) * 128, c0:c0 + n], writes=[st])
            en = cast_eng[ci % 3]
            if en == "act":
                P.op("act", lambda e, st=st, n=n, kc=kc, o0=o0: e.activation(out=wkv[:, kc, o0:o0 + n], in_=st[:, 0:n], func=AF.Copy),
                     reads=[st], writes=[wkv], nowaw=True)
            else:
                P.op(en, lambda e, st=st, n=n, kc=kc, o0=o0: e.tensor_copy(out=wkv[:, kc, o0:o0 + n], in_=st[:, 0:n]),
                     reads=[st], writes=[wkv], nowaw=True)
            ci += 1

    xt = [P1.sb("xt%d" % i, [128, D], F32) for i in range(2)]
    junk = P1.sb("junk", [128, D], BF16)
    ss = P1.sb("ss", [128, 1], F32)
    rstd = P1.sb("rstd", [128, 1], F32)
    xn = P1.sb("xn", [128, D], BF16)
    hT = P1.sb("hT", [128, 8, 128], BF16)
    nsa_tm = P1.sb("nsa_tm", [128, 768], BF16)
    pf = P1.sb("pf", [128, 768], F32)
    rt = [P1.sb("rt%d" % i, [128, 4 * 64], F32) for i in range(4)]
    k4 = [P1.sb("k4_%d" % i, [128, 512], BF16) for i in range(2)]
    vaug = [P1.sb("vaug%d" % i, [128, 2, 2, 66], BF16) for i in range(2)]
    krvr = [P1.sb("krvr%d" % i, [128, 1536], BF16) for i in range(2)]
    kd = P1.sb("kd", [128, 512], BF16)
    kvm = [P1.sb("kvm%d" % i, [128, 1024], BF16) for i in range(2)]
    for i in range(2):
        P.op("pool", lambda e, i=i: e.memset(vaug[i][:], 1.0), writes=[vaug[i]])

    W = dict(junk=junk, ss=ss, rstd=rstd, xn=xn, hT=hT, rt=rt)

    def rms_front(xtile, gvec_col, hT_out):
        junk, ss, rstd, xn = W["junk"], W["ss"], W["rstd"], W["xn"]
        P.op("act", lambda e: e.activation(out=junk[:], in_=xtile[:], func=AF.Square, accum_out=ss[:]),
             reads=[xtile], writes=[junk, ss])
        P.op("dve", lambda e: e.tensor_scalar(out=rstd[:], in0=ss[:], scalar1=1.0 / D, scalar2=1e-6,
                                              op0=ALU.mult, op1=ALU.add), reads=[ss], writes=[rstd])
        P.op("act", lambda e: e.activation(out=rstd[:], in_=rstd[:], func=AF.Ln), reads=[rstd], writes=[rstd])
        P.op("act", lambda e: e.activation(out=rstd[:], in_=rstd[:], func=AF.Exp, scale=-0.5), reads=[rstd], writes=[rstd])
        P.op("dve", lambda e: e.tensor_scalar(out=xn[:], in0=xtile[:], scalar1=rstd[:], scalar2=None, op0=ALU.mult),
             reads=[xtile, rstd], writes=[xn])
        for c in range(8):
            P.op("pe", lambda e, c=c: e.transpose(out=ps_t[:, c * 128:(c + 1) * 128], in_=xn[:, c * 128:(c + 1) * 128],
                                                  identity=identb[:]),
                 reads=[xn, identb], writes=[ps_t], nowaw=(c > 0))
        P.op("dve", lambda e: e.tensor_tensor(out=hT_out[:], in0=ps_t[:].rearrange("p (c t) -> p c t", c=8),
                                              in1=gvec_col[:].unsqueeze(2).to_broadcast([128, 8, 128]), op=ALU.mult),
             reads=[ps_t, gvec_col], writes=[hT_out])

    def proj(ps, ncols, w, wc0):
        hT = W["hT"]
        for kc in range(8):
            P.op("pe", lambda e, kc=kc: e.matmul(ps[:, 0:ncols], lhsT=hT[:, kc, :], rhs=w[:, kc, wc0:wc0 + ncols],
                                                 start=(kc == 0), stop=(kc == 7)),
                 reads=[hT, w], writes=[ps], nowaw=(kc > 0))

    def rotary(src, dst, nh, hd, half, cos_r, sin_r, G3=None):
        sb_, sap = src
        db_, dap = dst
        rt = W["rt"]
        if G3 is None:
            x1 = sap[:, :, 0:half]
            x2 = sap[:, :, half:2 * half]
            d1 = dap[:, :, 0:half]
            d2 = dap[:, :, half:2 * half]
            shp = [128, nh, half]
            cb = cos_r.unsqueeze(1).to_broadcast(shp)
            sbb = sin_r.unsqueeze(1).to_broadcast(shp)
            tv = [t[:, 0:nh * half].rearrange("p (h d) -> p h d", h=nh) for t in rt]
        else:
            a, b = G3
            x1 = sap[:, :, :, 0:half]
            x2 = sap[:, :, :, half:2 * half]
            d1 = dap[:, :, :, 0:half]
            d2 = dap[:, :, :, half:2 * half]
            shp = [128, a, b, half]
            cb = cos_r.unsqueeze(1).unsqueeze(1).to_broadcast(shp)
            sbb = sin_r.unsqueeze(1).unsqueeze(1).to_broadcast(shp)
            tv = [t[:, 0:a * b * half].rearrange("p (a b d) -> p a b d", a=a, b=b) for t in rt]
        tbl = [cosR, sinR, cosN, sinN] + W.get("ropex", [])
        P.op("dve", lambda e: e.tensor_tensor(out=tv[0], in0=x1, in1=cb, op=ALU.mult), reads=[sb_] + tbl, writes=[rt[0]])
        P.op("dve", lambda e: e.tensor_tensor(out=tv[1], in0=x2, in1=sbb, op=ALU.mult), reads=[sb_] + tbl, writes=[rt[1]])
        P.op("dve", lambda e: e.tensor_tensor(out=tv[2], in0=x2, in1=cb, op=ALU.mult), reads=[sb_] + tbl, writes=[rt[2]])
        P.op("dve", lambda e: e.tensor_tensor(out=tv[3], in0=x1, in1=sbb, op=ALU.mult), reads=[sb_] + tbl, writes=[rt[3]])
        P.op("dve", lambda e: e.tensor_tensor(out=d1, in0=tv[0], in1=tv[1], op=ALU.subtract),
             reads=[rt[0], rt[1]], writes=[db_], nowaw=False)
        P.op("dve", lambda e: e.tensor_tensor(out=d2, in0=tv[2], in1=tv[3], op=ALU.add),
             reads=[rt[2], rt[3]], writes=[db_], nowaw=True)

    KST = int(os.environ.get("KSTAGE", "9"))
    cosNa = P1.sb("cosNa", [128, 128, 8], F32)
    sinNa = P1.sb("sinNa", [128, 128, 8], F32)
    crs = [P1.sb("crs%d" % i, [128, 64], F32) for i in range(2)]
    srs = [P1.sb("srs%d" % i, [128, 64], F32) for i in range(2)]
    P.dma("sp", cosNa[:], cosna_d[:], writes=[cosNa])
    P.dma("sp", sinNa[:], sinna_d[:], writes=[sinNa])
    W["ropex"] = [cosNa, sinNa] + crs + srs
    junkB = P1.sb("junkB", [128, D], BF16)
    ssB = P1.sb("ssB", [128, 1], F32)
    rstdB = P1.sb("rstdB", [128, 1], F32)
    xnB = P1.sb("xnB", [128, D], BF16)
    hTB = P1.sb("hTB", [128, 8, 128], BF16)
    pfB = P1.sb("pfB", [128, 768], F32)
    nsa_tmB = P1.sb("nsa_tmB", [128, 768], BF16)
    kdB = P1.sb("kdB", [128, 512], BF16)
    rtB = [P1.sb("rtB%d" % i, [128, 4 * 64], F32) for i in range(4)]
    Wpar = [dict(junk=junk, ss=ss, rstd=rstd, xn=xn, hT=hT, rt=rt), dict(junk=junkB, ss=ssB, rstd=rstdB, xn=xnB, hT=hTB, rt=rtB)]
    bufpar = [(hT, pf, nsa_tm, kd, ps_a, ps_b), (hTB, pfB, nsa_tmB, kdB, ps_e, ps_p)]

    def p1_tile(it, hT, pf, nsa_tm, kd, ps_a, ps_b, phase):
        own = it >= 128
        r = it - 128 if own else it
        xb = xt[it % 2]
        if own:
            if phase == 0:
                P.dma("sp", xb[:], x_d[r], writes=[xb])
            cN_ap = sN_ap = None
            cR_ap, sR_ap = cosR[:, r, :], sinR[:, r, :]
        else:
            if phase == 0:
                P.dma("sp", xb[:], xall_d[r], writes=[xb])
                P.dma("sp", crs[it % 2][:], cosra_d[r], writes=[crs[it % 2]])
                P.dma("sp", srs[it % 2][:], sinra_d[r], writes=[srs[it % 2]])
            cN_ap, sN_ap = cosNa[:, r, :], sinNa[:, r, :]
            cR_ap, sR_ap = crs[it % 2][:], srs[it % 2][:]
        KSUB = int(os.environ.get("KSUB", "9"))
        if phase == 0:
            rms_front(xb, gcol, hT)
            return
        if KSUB < 2:
            return
        nsa_on = not own
        if nsa_on:
          proj(ps_a, 512, wkv, 0)
          proj(ps_b, 256, wkv, 512)
          P.op("act", lambda e: e.activation(out=pf[:, 0:512], in_=ps_a[:, 0:512], func=AF.Copy),
               reads=[ps_a], writes=[pf])
          P.op("act", lambda e: e.activation(out=pf[:, 512:768], in_=ps_b[:, 0:256], func=AF.Copy),
               reads=[ps_b], writes=[pf], nowaw=True)
          P.op("dve", lambda e: e.tensor_copy(out=nsa_tm[:], in_=pf[:]), reads=[pf], writes=[nsa_tm])
          pf5 = pf[:, 0:768].rearrange("p (a x b d) -> p a x b d", a=3, x=2, b=2)[:, :, 0, :, :]
          nt5 = nsa_tm[:, 0:768].rearrange("p (a x b d) -> p a x b d", a=3, x=2, b=2)[:, :, 0, :, :]
          rotary((pf, pf5), (nsa_tm, nt5), None, 64, 8, cN_ap, sN_ap, G3=(3, 2))
          vb = vaug[it % 2]
          P.op("dve", lambda e, vb=vb: e.tensor_copy(out=vb[:, 0, :, 0:64],
                                                     in_=nsa_tm[:, 384:512].rearrange("p (g d) -> p g d", g=2)),
               reads=[nsa_tm], writes=[vb])
          P.op("dve", lambda e, vb=vb: e.tensor_copy(out=vb[:, 1, :, 0:64],
                                                     in_=nsa_tm[:, 640:768].rearrange("p (g d) -> p g d", g=2)),
               reads=[nsa_tm], writes=[vb], nowaw=True)
          P.dma("pool", gv[r * 128:(r + 1) * 128, :], vb[:].rearrange("p a g d -> p (a g d)"),
                reads=[vb], writes=[gv], nowaw=True)
        if KSUB < 6:
            return
        kb = krvr[it % 2]
        proj(ps_d, 512, wkv, 768)
        P.op("act", lambda e: e.activation(out=pf[:, 0:512], in_=ps_d[:, 0:512], func=AF.Copy),
             reads=[ps_d], writes=[pf])
        rotary((pf, pf[:, 0:512].rearrange("p (h d) -> p h d", h=4)),
               (kb, kb[:, 0:512].rearrange("p (h d) -> p h d", h=4)),
               4, 128, 64, cR_ap, sR_ap)
        proj(ps_a, 512, wkv, 768 + 512)
        proj(ps_b, 512, wkv, 768 + 1024)
        P.op("act", lambda e, kb=kb: e.activation(out=kb[:, 512:1024], in_=ps_a[:, 0:512], func=AF.Copy),
             reads=[ps_a], writes=[kb], nowaw=True)
        P.op("act", lambda e, kb=kb: e.activation(out=kb[:, 1024:1536], in_=ps_b[:, 0:512], func=AF.Copy),
             reads=[ps_b], writes=[kb], nowaw=True)
        if own:
            P.dma("pool", lkv[r * 128:(r + 1) * 128, :], kb[:], reads=[kb], writes=[lkv], nowaw=True)
            return
        if nsa_on:
          for i, c0 in enumerate((0, 128, 256, 512)):
              P.op("pe", lambda e, i=i, c0=c0: e.transpose(out=ps_t[:, i * 128:(i + 1) * 128], in_=nsa_tm[:, c0:c0 + 128],
                                                           identity=identb[:]),
                   reads=[nsa_tm, identb], writes=[ps_t], nowaw=(i > 0))
          k4b = k4[it % 2]
          P.op("act", lambda e, k4b=k4b: e.activation(out=k4b[:], in_=ps_t[:, 0:512], func=AF.Copy),
               reads=[ps_t], writes=[k4b])
          P.dma("pool", g4[r * 128:(r + 1) * 128, :], k4b[:], reads=[k4b], writes=[g4], nowaw=True)
        P.op("dve", lambda e, kb=kb: e.tensor_tensor(out=kd[:].rearrange("p (h d) -> p h d", h=4),
                                                     in0=kb[:, 0:512].rearrange("p (h d) -> p h d", h=4),
                                                     in1=kdec[:].unsqueeze(2).to_broadcast([128, 4, 128]), op=ALU.mult),
             reads=[kb, kdec], writes=[kd])
        for h in range(4):
            P.op("pe", lambda e, h=h, kb=kb: e.matmul(ps_c[:, h * 256:(h + 1) * 256], lhsT=kd[:, h * 128:(h + 1) * 128],
                                                      rhs=kb[:, 512 + h * 256:512 + (h + 1) * 256], start=True, stop=True),
                 reads=[kd, kb], writes=[ps_c0, ps_c1], nowaw=(h > 0))
        kvb = kvm[it % 2]
        P.op("act", lambda e, kvb=kvb: e.activation(out=kvb[:, 0:512], in_=ps_c[:, 0:512], func=AF.Copy),
             reads=[ps_c0, ps_c1], writes=[kvb])
        P.op("dve", lambda e, kvb=kvb: e.tensor_copy(out=kvb[:, 512:1024], in_=ps_c[:, 512:1024]),
             reads=[ps_c0, ps_c1], writes=[kvb], nowaw=True)
        P.dma("pool", gkv[r * 128:(r + 1) * 128, :], kvb[:], reads=[kvb], writes=[gkv], nowaw=True)


    n_p1 = (128 + NR) if KST >= 2 else 0
    if n_p1:
        W.update(Wpar[0])
        p1_tile(0, *bufpar[0], phase=0)
    for it in range(n_p1):
        if it + 1 < n_p1:
            W.update(Wpar[(it + 1) % 2])
            p1_tile(it + 1, *bufpar[(it + 1) % 2], phase=0)
        W.update(Wpar[it % 2])
        p1_tile(it, *bufpar[it % 2], phase=1)
    W.update(Wpar[0])
    P.barrier()
    es1.close()

    if debug and os.environ.get("KDUMP1", "0") == "1":
        P.dma("sp", dbg["g4"][:], g4[:], reads=[g4], writes=[dbg["g4"]])
        P.dma("sp", dbg["gv"][:], gv[:], reads=[gv], writes=[dbg["gv"]])
        P.dma("sp", dbg["gkv"][:], gkv[:], reads=[gkv], writes=[dbg["gkv"]])
        P.dma("sp", dbg["lkv"][:], lkv[:], reads=[lkv], writes=[dbg["lkv"]])

    kcT = P.sb("kcT", [128, 1024], BF16)
    vcS = P.sb("vcS", [128, 8, 128], BF16)
    es2 = ExitStack()
    P2 = Prog.__new__(Prog)
    P2.__dict__ = P.__dict__.copy()
    P2.es = es2
    KVT = P2.sb("KVT", [128, S], BF16)
    w1T = P2.sb("w1T", [128, 32, 256], BF16)
    w1stg = P2.sb("w1stg", [128, 8, 256], F32)
    w2stg = P2.sb("w2stg", [128, 2, 64], F32)
    w2pad = P2.sb("w2pad", [128, 2, 2, 128], BF16)
    w2s = P2.sb("w2s", [128, 2, 64], BF16)
    pestg = P2.sb("pestg", [128, 32], F32)
    peT = P2.sb("peT", [128, 32], BF16)
    b1 = P2.sb("b1", [128, 2], F32)
    h1T = P2.sb("h1T", [128, 2, 2, 1024], BF16)
    vcT = P2.sb("vcT", [128, 1024], BF16)
    zt = P2.sb("zt", [128, 512], F32)
    tt = P2.sb("tt", [128, 512], F32)
    et = P2.sb("et", [128, 512], F32)
    P.op("pool", lambda e: e.memset(h1T[:], 0.0), writes=[h1T])
    P.op("pool", lambda e: e.memset(w2pad[:], 0.0), writes=[w2pad])
    GC = float(2.0 * np.sqrt(2.0 / np.pi))
    for kv_i, (w1_d, w2_d, pe_d) in enumerate(((w1k_d, w2k_d, pek_d), (w1v_d, w2v_d, pev_d))):
        for b in range(128):
            sl = slot_of(b)
            P.dma("sp", KVT[:, b * 128:(b + 1) * 128], g4[sl * 128:(sl + 1) * 128, kv_i * 128:(kv_i + 1) * 128],
                  reads=[g4], writes=[KVT], nowaw=(b > 0))
        w1v = w1_d.t.rearrange("(l d) h -> d l h", d=64)
        for part in range(4):
            for half in range(2):
                P.dma("sp", w1stg[half * 64:(half + 1) * 64, :, :], w1v[:, part * 8:(part + 1) * 8, :],
                      writes=[w1stg], nowaw=(half > 0))
            P.op("dve", lambda e, part=part: e.tensor_copy(out=w1T[:, part * 8:(part + 1) * 8, :], in_=w1stg[:]),
                 reads=[w1stg], writes=[w1T], nowaw=(part > 0))
        P.dma("sp", w2stg[:], w2_d.t.rearrange("(hh p) d -> p hh d", p=128), writes=[w2stg])
        P.op("dve", lambda e: e.tensor_copy(out=w2s[:], in_=w2stg[:]), reads=[w2stg], writes=[w2s])
        for g in range(2):
            P.op("dve", lambda e, g=g: e.tensor_copy(out=w2pad[:, :, g, g * 64:(g + 1) * 64], in_=w2stg[:]),
                 reads=[w2stg], writes=[w2pad], nowaw=(g > 0))
        for half in range(2):
            P.dma("sp", pestg[half * 64:(half + 1) * 64, :], pe_d.t.rearrange("l d -> d l"), writes=[pestg],
                  nowaw=(half > 0), slow=True)
        P.op("dve", lambda e: e.tensor_copy(out=peT[:], in_=pestg[:]), reads=[pestg], writes=[peT])
        KCMP = int(os.environ.get("KCMP", "9"))
        if KCMP < 2:
            continue
        for hh in range(2):
            for l in range(32):
                P.op("pe", lambda e, hh=hh, l=l: e.matmul(ps_p[:, hh:hh + 1], lhsT=w1T[0:64, l, hh * 128:(hh + 1) * 128],
                                                          rhs=peT[0:64, l:l + 1], start=(l == 0), stop=(l == 31)),
                     reads=[w1T, peT], writes=[ps_p], nowaw=not (hh == 0 and l == 0))
        P.op("dve", lambda e: e.tensor_copy(out=b1[:], in_=ps_p[:, 0:2]), reads=[ps_p], writes=[b1])
        if KCMP < 3:
            continue
        KV3 = KVT[:].rearrange("p (c s) -> p c s", s=16)
        pss = [ps_a, ps_b]
        it = 0
        for g in range(2):
            for hh in range(2):
                for ch in range(2):
                    c0 = 512 * ch
                    n = 512 if ch == 0 else 511
                    psx = pss[it % 2]
                    it += 1
                    for l in range(32):
                        cs = c0 + (l // 16)
                        P.op("pe", lambda e, g=g, hh=hh, l=l, cs=cs, n=n, psx=psx: e.matmul(
                            psx[:, 0:n], lhsT=w1T[g * 64:(g + 1) * 64, l, hh * 128:(hh + 1) * 128],
                            rhs=KV3[g * 64:(g + 1) * 64, cs:cs + n, l % 16], start=(l == 0), stop=(l == 31)),
                            reads=[w1T, KVT], writes=[psx], nowaw=(l > 0))
                    if KCMP < 4:
                        continue
                    P.op("dve", lambda e, psx=psx, n=n, hh=hh: e.tensor_scalar(out=zt[:, 0:n], in0=psx[:, 0:n], scalar1=b1[:, hh:hh + 1],
                                                                                scalar2=None, op0=ALU.add),
                         reads=[psx, b1], writes=[zt])
                    P.op("pool", lambda e, n=n: e.tensor_tensor(out=tt[:, 0:n], in0=zt[:, 0:n], in1=zt[:, 0:n], op=ALU.mult),
                         reads=[zt], writes=[tt])
                    P.op("pool", lambda e, n=n: e.tensor_scalar(out=tt[:, 0:n], in0=tt[:, 0:n], scalar1=0.044715, scalar2=1.0,
                                                                op0=ALU.mult, op1=ALU.add), reads=[tt], writes=[tt])
                    P.op("pool", lambda e, n=n: e.tensor_tensor(out=tt[:, 0:n], in0=tt[:, 0:n], in1=zt[:, 0:n], op=ALU.mult),
                         reads=[tt, zt], writes=[tt])
                    P.op("act", lambda e, n=n: e.activation(out=et[:, 0:n], in_=tt[:, 0:n], func=AF.Exp, scale=-GC),
                         reads=[tt], writes=[et])
                    P.op("dve", lambda e, n=n: e.tensor_scalar(out=et[:, 0:n], in0=et[:, 0:n], scalar1=1.0, scalar2=None, op0=ALU.add),
                         reads=[et], writes=[et])
                    P.op("dve", lambda e, n=n: e.reciprocal(out=et[:, 0:n], in_=et[:, 0:n]), reads=[et], writes=[et])
                    P.op("dve", lambda e, n=n, hh=hh, g=g, c0=c0: e.tensor_tensor(out=h1T[:, hh, g, c0:c0 + n], in0=zt[:, 0:n], in1=et[:, 0:n],
                                                                                 op=ALU.mult),
                         reads=[zt, et], writes=[h1T], nowaw=True)
        if KCMP < 5:
            continue
        dstT = kcT if kv_i == 0 else vcT
        for ch in range(2):
            c0 = 512 * ch
            k = 0
            for g in range(2):
                for hh in range(2):
                    P.op("pe", lambda e, g=g, hh=hh, c0=c0, k=k: e.matmul(ps_d[:, 0:512], lhsT=w2pad[:, hh, g, :],
                                                                         rhs=h1T[:, hh, g, c0:c0 + 512], start=(k == 0), stop=(k == 3)),
                         reads=[w2pad, h1T], writes=[ps_d], nowaw=(k > 0))
                    k += 1
            P.op("act", lambda e, c0=c0, dstT=dstT: e.activation(out=dstT[:, c0:c0 + 512], in_=ps_d[:, 0:512], func=AF.Copy),
                 reads=[ps_d], writes=[dstT], nowaw=(ch > 0))
        if kv_i == 1:
            for cc in range(8):
                P.op("pe", lambda e, cc=cc: e.transpose(out=ps_t[:, cc * 128:(cc + 1) * 128], in_=vcT[:, cc * 128:(cc + 1) * 128],
                                                        identity=identb[:]),
                     reads=[vcT, identb], writes=[ps_t], nowaw=(cc > 0))
            P.op("dve", lambda e: e.tensor_copy(out=vcS[:], in_=ps_t[:].rearrange("p (c f) -> p c f", c=8)),
                 reads=[ps_t], writes=[vcS])
    P.barrier()
    es2.close()
    if debug:
        P.dma("sp", dbg["kcT"][:], kcT[:], reads=[kcT], writes=[dbg["kcT"]])
        P.dma("sp", dbg["vcS"][:], vcS[:], reads=[vcS], writes=[dbg["vcS"]])

    es3 = ExitStack()
    P3 = Prog.__new__(Prog)
    P3.__dict__ = P.__dict__.copy()
    P3.es = es3
    KP2 = int(os.environ.get("KP2", "99"))
    xt3 = P3.sb("xt3", [128, D], F32)
    W["ropex"] = []
    W.update(junk=P3.sb("junk3", [128, D], BF16), ss=P3.sb("ss3", [128, 1], F32), rstd=P3.sb("rstd3", [128, 1], F32),
             xn=P3.sb("xn3", [128, D], BF16), hT=P3.sb("hT3", [128, 8, 128], BF16),
             rt=[P3.sb("rt3_%d" % i, [128, 256], F32) for i in range(4)])
    hT = W["hT"]
    pf2a = P3.sb("pf3", [128, 512], F32)
    QN, GN, ZN, QR, ZR = 0, 512, 536, 1048, 1560
    wq = P3.sb("wq", [128, 8, 2584], BF16)
    stg2 = P3.sb("wstgb", [128, 1048], F32)
    ci = 0
    for kc in range(8):
        for (c0, n, o0) in ((0, 512, 0), (1280, 1048, 512), (3864, 1024, 1560)):
            P.dma("sp", stg2[:, 0:n], w_in_d[kc * 128:(kc + 1) * 128, c0:c0 + n], writes=[stg2])
            en = cast_eng[ci % 3]
            if en == "act":
                P.op("act", lambda e, n=n, kc=kc, o0=o0: e.activation(out=wq[:, kc, o0:o0 + n], in_=stg2[:, 0:n], func=AF.Copy),
                     reads=[stg2], writes=[wq], nowaw=True)
            else:
                P.op(en, lambda e, n=n, kc=kc, o0=o0: e.tensor_copy(out=wq[:, kc, o0:o0 + n], in_=stg2[:, 0:n]),
                     reads=[stg2], writes=[wq], nowaw=True)
            ci += 1
    e32 = P3.sb("e32", [128, 32, 128], BF16)
    wb = P3.sb("wb", [128, 2, 12, 128], BF16)
    dbm = P3.sb("dbm", [128, 2, 8, 128], BF16)
    cb = P3.sb("cb", [128, 2, 72], F32)
    iota = P3.sb("iota", [128, 256], F32)
    cur = P3.sb("cur", [128, NR, 2], F32)
    qdec = P3.sb("qdec", [128, 4], F32)
    dmask = P3.sb("dmask", [128, 4, 128], F32)
    wret = P3.sb("wret", [128, NR, 16, 4], F32)
    dfac = P3.sb("dfac", [128, NR, 4], F32)
    bgate = P3.sb("bgate", [128, 24], F32)
    for t, d in ((e32, e32_d), (wb, wb_d), (dbm, db_d), (cb, cb_d), (iota, iota_d), (cur, cur_d), (qdec, qdec_d),
                 (dmask, dmask_d), (wret, wret_d), (dfac, dfac_d)):
        P.dma("sp", t[:], d[:], writes=[t])
    P.dma("sp", bgate[:], bgate_d.t.partition_broadcast(128), writes=[bgate])

    Rst = P3.sb("Rst", [128, 1024], F32)
    Pc = P3.sb("Pc", [128, 1024], BF16)
    PcT = P3.sb("PcT", [128, 1024], BF16)
    pg = [P3.sb("pg%d" % g, [128, 1024], F32) for g in range(2)]
    P.op("pool", lambda e: e.memset(Rst[:], 0.0), writes=[Rst])
    P.op("pool", lambda e: e.memset(Pc[:], 0.0), writes=[Pc])
    for g in range(2):
        P.op("pool", lambda e, g=g: e.memset(pg[g][:], 0.0), writes=[pg[g]])
    qn_tm = P3.sb("qn_tm", [128, 512], BF16)
    qT = P3.sb("qT", [128, 4, 128], BF16)
    gt = P3.sb("gt", [128, 24], F32)
    ez = P3.sb("ez", [128, 1024], F32)
    szn = P3.sb("szn", [128, 512], F32)
    oa = P3.sb("oa", [128, 8, 64], F32)
    rs = P3.sb("rs", [128, 8], F32)
    rinv = P3.sb("rinv", [128, 8], F32)
    sc8 = P3.sb("sc8", [128, 8], F32)
    m1 = P3.sb("m1", [128, 256], F32)
    m2 = P3.sb("m2", [128, 256], F32)
    ta = P3.sb("ta", [128, 256], F32)
    s4 = P3.sb("s4", [128, 256], F32)
    score = P3.sb("score", [128, 256], F32)
    rep = P3.sb("rep", [128, 256], F32)
    mx = P3.sb("mx", [128, 16], F32)
    thr = P3.sb("thr", [128, 1], F32)
    biastm = P3.sb("biastm", [128, 256], BF16)
    BT8 = [[P3.sb("BT8_%d_%d" % (g, t), [128, 2, 128], BF16) for t in range(2)] for g in range(2)]
    for g in range(2):
        for t in range(2):
            P.op("pool", lambda e, g=g, t=t: e.memset(BT8[g][t][:], 0.0), writes=[BT8[g][t]])
    K8 = [[P3.sb("K8_%d_%d" % (i, g), [128, 8, 128], BF16) for g in range(2)] for i in range(2)]
    for i in range(2):
        for g in range(2):
            P.op("pool", lambda e, i=i, g=g: e.memset(K8[i][g][:], 0.0), writes=[K8[i][g]])
    V8 = [P3.sb("V8_%d" % i, [128, 8, 132], BF16) for i in range(2)]
    eT = [P3.sb("eT%d" % i, [128, 512], BF16) for i in range(4)]
    accsb = P3.sb("accsb", [128, 512], F32)
    den4 = P3.sb("den4", [128, 4], F32)
    s4g = P3.sb("s4g", [128, 4], F32)
    A1 = P3.sb("A1", [128, 512], BF16)
    qr = P3.sb("qr", [128, 512], BF16)
    qd = P3.sb("qd", [128, 512], BF16)
    krvr2 = P3.sb("krvr2", [128, 1536], BF16)
    qkT = P3.sb("qkT", [128, 8, 128], BF16)
    qdT = P3.sb("qdT", [128, 4, 128], BF16)
    attT = P3.sb("attT", [128, 4, 128], BF16)
    kvb = [P3.sb("kvb%d" % i, [128, 1024], BF16) for i in range(2)]
    tmpkv = P3.sb("tmpkv", [128, 1024], F32)
    Rb = P3.sb("Rb", [128, 1024], BF16)
    ob = P3.sb("ob", [128, 1024], F32)
    sm4 = P3.sb("sm4", [128, 4], F32)
    B1 = P3.sb("B1", [128, 1024], BF16)

    g4v = g4.t.rearrange("(c rr p) f -> p c rr f", c=8, rr=NR, p=128)
    gvv = gv.t.rearrange("(c rr p) f -> p c rr f", c=8, rr=NR, p=128)
    gate3 = gt[:].rearrange("p (h t) -> p h t", t=3)
    pss = [ps_a, ps_b]
    acc = [ps_d, ps_e]
    psl = [(ps_a, ps_a[:, 0:512]), (ps_b, ps_b[:, 0:512]), (ps_c0, ps_c[:, 0:512]), (ps_c1, ps_c[:, 512:1024])]

    def attn_branch(r, chunks, kcol, vcol, br):
        par = r % 2
        Rs = sorted(set(kk // 8 for kk, _ in chunks))
        steps = []
        for n_, R in enumerate(Rs):
            kb_, vb_ = K8[n_ % 2], V8[n_ % 2]
            for kk, bf_ in [(kk, bf_) for kk, bf_ in chunks if kk // 8 == R]:
                j = kk % 8
                c = j if R % 2 == 0 else 7 - j
                first = (kk == chunks[0][0])
                last = (kk == chunks[-1][0])
                for g in range(2):
                    extra = []
                    if br == 1:
                        t2_ = (kk % 64) // 32
                        extra.append((e32, e32[:, kk % 32, :], BT8[g][t2_],
                                      BT8[g][t2_][:, kk // 64, :].unsqueeze(1).to_broadcast([128, 4, 128])))
                    if bf_ is not None:
                        extra.append(bf_)
                    si = len(steps)
                    steps.append(dict(R=R, kb_=kb_, vb_=vb_, c=c, g=g, extra=extra, first=first, last=last,
                                      psx=psl[si % 4], et_=eT[si % 4]))
        loaded = set()

        def emit_scores(st):
            R, kb_, vb_, c, g, extra, psx, et_ = st["R"], st["kb_"], st["vb_"], st["c"], st["g"], st["extra"], st["psx"], st["et_"]
            if R not in loaded:
                loaded.add(R)
                for g2 in range(2):
                    P.dma("sp", kb_[g2][g2 * 64:(g2 + 1) * 64, :, :], g4v[g2 * 64:(g2 + 1) * 64, :, R, kcol:kcol + 128],
                          reads=[g4], writes=[kb_[g2]])
                P.dma("sp", vb_[:], gvv[:, :, R, vcol:vcol + 132], reads=[gv], writes=[vb_])
            psb, psap = psx
            P.op("pe", lambda e, psap=psap, kb_=kb_, c=c, g=g, ne=len(extra): e.matmul(
                psap, lhsT=kb_[g][:, c, :], rhs=qT[:, :, :],
                start=True, stop=(ne == 0)), reads=[kb_[g], qT], writes=[psb])
            for xi, (lb, lap, rb_, rap) in enumerate(extra):
                P.op("pe", lambda e, psap=psap, lap=lap, rap=rap, lst=(xi == len(extra) - 1): e.matmul(
                    psap, lhsT=lap, rhs=rap, start=False, stop=lst), reads=[lb, rb_], writes=[psb], nowaw=True)
            P.op("act", lambda e, psap=psap, et_=et_: e.activation(out=et_[:], in_=psap, func=AF.Exp),
                 reads=[psb], writes=[et_])

        def emit_pv(st):
            g, vb_, c, et_, first, last = st["g"], st["vb_"], st["c"], st["et_"], st["first"], st["last"]
            P.op("pe", lambda e, g=g, vb_=vb_, c=c, et_=et_, first=first, last=last: e.matmul(
                acc[g][0:65, 0:512], lhsT=vb_[:, c, g * 66:g * 66 + 65], rhs=et_[:], start=first, stop=last),
                reads=[vb_, et_], writes=[acc[g]], nowaw=not first)

        AHEAD = 3
        for si in range(min(AHEAD, len(steps))):
            emit_scores(steps[si])
        for si in range(len(steps)):
            if si + AHEAD < len(steps):
                emit_scores(steps[si + AHEAD])
            emit_pv(steps[si])
        KAB = int(os.environ.get("KAB", "9"))
        for g in range(2 if KAB >= 2 else 0):
            P.op("act", lambda e, g=g: e.activation(out=accsb[0:65, :], in_=acc[g][0:65, 0:512], func=AF.Copy),
                 reads=[acc[g]], writes=[accsb])
            for hh in range(4):
                P.op("pe", lambda e, hh=hh: e.transpose(out=ps_p[:, hh * 65:(hh + 1) * 65], in_=accsb[0:65, hh * 128:(hh + 1) * 128],
                                                        identity=identf[0:65, 0:65]),
                     reads=[accsb, identf], writes=[ps_p], nowaw=(hh > 0))
            pv = ps_p[:, 0:260].rearrange("p (h f) -> p h f", f=65)
            P.op("dve", lambda e, pv=pv: e.tensor_scalar(out=den4[:], in0=pv[:, :, 64], scalar1=1e-30, scalar2=None, op0=ALU.max),
                 reads=[ps_p], writes=[den4])
            P.op("dve", lambda e: e.reciprocal(out=den4[:], in_=den4[:]), reads=[den4], writes=[den4])
            P.op("dve", lambda e, g=g: e.tensor_tensor(out=s4g[:], in0=den4[:], in1=gate3[:, g * 4:(g + 1) * 4, br], op=ALU.mult),
                 reads=[den4, gt], writes=[s4g])
            for hh in range(4):
                P.op("dve", lambda e, g=g, hh=hh: e.scalar_tensor_tensor(out=oa[:, g * 4 + hh, :], in0=ps_p[:, hh * 65:hh * 65 + 64],
                                                                         scalar=s4g[:, hh:hh + 1], in1=oa[:, g * 4 + hh, :],
                                                                         op0=ALU.mult, op1=ALU.add),
                     reads=[ps_p, s4g, oa], writes=[oa])

    KR = int(os.environ.get("KR", "99"))
    for r in range(min(NR, KR) if KP2 >= 1 else 0):
        par = r % 2
        P.dma("sp", xt3[:], x_d[r], writes=[xt3])
        rms_front(xt3, gcol, hT)
        Rst3 = Rst[:].rearrange("p (h e) -> p h e", h=4)
        P.op("pool", lambda e, r=r: e.tensor_tensor(out=Rst3, in0=Rst3, in1=dfac[:, r, :].unsqueeze(2).to_broadcast([128, 4, 256]), op=ALU.mult),
             reads=[Rst, dfac], writes=[Rst])
        nk = 0
        for mi in range(16):
            m = 8 * (r - 1) + mi
            if m < 0:
                continue
            kb2 = kvb[nk % 2]
            nk += 1
            sl = slot_of(m)
            P.dma("sp", kb2[:], gkv[sl * 128:(sl + 1) * 128, :], reads=[gkv], writes=[kb2])
            P.op("pool", lambda e, kb2=kb2, mi=mi, r=r: e.tensor_tensor(out=tmpkv[:].rearrange("p (h e) -> p h e", h=4),
                                                                   in0=kb2[:].rearrange("p (h e) -> p h e", h=4),
                                                                   in1=wret[:, r, mi, :].unsqueeze(2).to_broadcast([128, 4, 256]), op=ALU.mult),
                 reads=[kb2, wret], writes=[tmpkv])
            P.op("pool", lambda e: e.tensor_tensor(out=Rst[:], in0=Rst[:], in1=tmpkv[:], op=ALU.add), reads=[Rst, tmpkv], writes=[Rst])
        P.op("dve", lambda e: e.tensor_copy(out=Rb[:], in_=Rst[:]), reads=[Rst], writes=[Rb])
        proj(ps_a, 512, wq, QN)
        P.op("act", lambda e: e.activation(out=pf2a[:, 0:512], in_=ps_a[:, 0:512], func=AF.Copy, scale=0.125),
             reads=[ps_a], writes=[pf2a])
        pf4 = pf2a[:, 0:512].rearrange("p (g hh d) -> p g hh d", g=2, hh=4)
        qn4 = qn_tm[:].rearrange("p (hh g d) -> p g hh d", hh=4, g=2)
        P.op("dve", lambda e: e.tensor_copy(out=qn4, in_=pf4), reads=[pf2a], writes=[qn_tm])
        rotary((pf2a, pf4), (qn_tm, qn4), None, 64, 8, cosN[:, r, :], sinN[:, r, :], G3=(2, 4))
        for hh in range(4):
            P.op("pe", lambda e, hh=hh: e.transpose(out=ps_t[:, hh * 128:(hh + 1) * 128], in_=qn_tm[:, hh * 128:(hh + 1) * 128],
                                                    identity=identb[:]),
                 reads=[qn_tm, identb], writes=[ps_t], nowaw=(hh > 0))
        P.op("dve", lambda e: e.tensor_copy(out=qT[:], in_=ps_t[:, 0:512].rearrange("p (h t) -> p h t", h=4)),
             reads=[ps_t], writes=[qT])
        proj(ps_b, 24, wq, GN)
        P.op("dve", lambda e: e.tensor_tensor(out=gt[:], in0=ps_b[:, 0:24], in1=bgate[:], op=ALU.add),
             reads=[ps_b, bgate], writes=[gt])
        P.op("act", lambda e: e.activation(out=gt[:], in_=gt[:], func=AF.Exp, scale=-1.0), reads=[gt], writes=[gt])
        P.op("dve", lambda e: e.tensor_scalar(out=gt[:], in0=gt[:], scalar1=1.0, scalar2=None, op0=ALU.add), reads=[gt], writes=[gt])
        P.op("dve", lambda e: e.reciprocal(out=gt[:], in_=gt[:]), reads=[gt], writes=[gt])
        proj(ps_b, 512, wq, ZN)
        P.op("act", lambda e: e.activation(out=ez[:, 0:512], in_=ps_b[:, 0:512], func=AF.Exp, scale=-1.0), reads=[ps_b], writes=[ez])
        P.op("dve", lambda e: e.tensor_scalar(out=ez[:, 0:512], in0=ez[:, 0:512], scalar1=1.0, scalar2=None, op0=ALU.add),
             reads=[ez], writes=[ez])
        P.op("dve", lambda e: e.reciprocal(out=ez[:, 0:512], in_=ez[:, 0:512]), reads=[ez], writes=[ez])
        P.op("dve", lambda e: e.tensor_tensor(out=szn[:], in0=ps_b[:, 0:512], in1=ez[:, 0:512], op=ALU.mult),
             reads=[ps_b, ez], writes=[szn])
        if KP2 < 2:
            continue
        Nc = 64 * (r + 1)
        nch = (Nc + 127) // 128
        P.op("dve", lambda e: e.memset(rs[:], 0.0), writes=[rs])
        for h in range(8):
            g, hh = h // 4, h % 4
            for p0 in range(0, Nc, 512):
                p1 = min(Nc, p0 + 512)
                P.op("pe", lambda e, g=g, hh=hh, p0=p0, p1=p1: e.matmul(ps_c[:, p0:p1], lhsT=qT[g * 64:(g + 1) * 64, hh, :],
                                                                       rhs=kcT[g * 64:(g + 1) * 64, p0:p1], start=True, stop=True),
                     reads=[qT, kcT], writes=[ps_c0, ps_c1], nowaw=(p0 > 0))
            lo = max(0, Nc - 72)
            wd = Nc - lo
            P.op("dve", lambda e, lo=lo, wd=wd, Nc=Nc, par=par: e.tensor_tensor(out=ps_c[:, lo:Nc], in0=ps_c[:, lo:Nc], in1=cb[:, par, 72 - wd:72],
                                                                      op=ALU.add), reads=[ps_c0, ps_c1, cb], writes=[ps_c0, ps_c1])
            P.op("act", lambda e, h=h, Nc=Nc: e.activation(out=Pc[:, 0:Nc], in_=ps_c[:, 0:Nc], func=AF.Exp, accum_out=rs[:, h:h + 1]),
                 reads=[ps_c0, ps_c1], writes=[Pc, rs])
            P.op("dve", lambda e, h=h: e.tensor_scalar(out=rinv[:, h:h + 1], in0=rs[:, h:h + 1], scalar1=1e-30, scalar2=None, op0=ALU.max),
                 reads=[rs], writes=[rinv])
            P.op("dve", lambda e, h=h: e.reciprocal(out=rinv[:, h:h + 1], in_=rinv[:, h:h + 1]), reads=[rinv], writes=[rinv])
            if hh == 0:
                P.op("dve", lambda e, h=h, g=g, Nc=Nc: e.tensor_scalar(out=pg[g][:, 0:Nc], in0=Pc[:, 0:Nc], scalar1=rinv[:, h:h + 1],
                                                                        scalar2=None, op0=ALU.mult), reads=[Pc, rinv], writes=[pg[g]])
            else:
                P.op("dve", lambda e, h=h, g=g, Nc=Nc: e.scalar_tensor_tensor(out=pg[g][:, 0:Nc], in0=Pc[:, 0:Nc], scalar=rinv[:, h:h + 1],
                                                                               in1=pg[g][:, 0:Nc], op0=ALU.mult, op1=ALU.add),
                     reads=[Pc, rinv, pg[g]], writes=[pg[g]])
            for j in range(nch):
                P.op("pe", lambda e, j=j: e.transpose(out=ps_t[:, j * 128:(j + 1) * 128], in_=Pc[:, j * 128:(j + 1) * 128], identity=identb[:]),
                     reads=[Pc, identb], writes=[ps_t], nowaw=(j > 0))
            P.op("act", lambda e, nch=nch: e.activation(out=PcT[:, 0:nch * 128], in_=ps_t[:, 0:nch * 128], func=AF.Copy),
                 reads=[ps_t], writes=[PcT])
            for j in range(nch):
                P.op("pe", lambda e, j=j, h=h, g=g, nch=nch: e.matmul(ps_p[:, h * 64:(h + 1) * 64], lhsT=PcT[:, j * 128:(j + 1) * 128],
                                                                     rhs=vcS[:, j, g * 64:(g + 1) * 64], start=(j == 0), stop=(j == nch - 1)),
                     reads=[PcT, vcS], writes=[ps_p], nowaw=not (h == 0 and j == 0))
        P.op("dve", lambda e: e.tensor_tensor(out=sc8[:], in0=rinv[:], in1=gate3[:, :, 0], op=ALU.mult), reads=[rinv, gt], writes=[sc8])
        P.op("dve", lambda e: e.tensor_tensor(out=oa[:], in0=ps_p[:, 0:512].rearrange("p (h d) -> p h d", h=8),
                                              in1=sc8[:].unsqueeze(2).to_broadcast([128, 8, 64]), op=ALU.mult),
             reads=[ps_p, sc8], writes=[oa])
        if KP2 < 3:
            continue
        P.op("dve", lambda e, r=r: e.tensor_scalar(out=m1[:], in0=iota[:], scalar1=cur[:, r, 0:1], scalar2=None, op0=ALU.is_lt),
             reads=[iota, cur], writes=[m1])
        P.op("dve", lambda e: e.memset(m1[:, 0:1], 0.0), writes=[m1])
        P.op("dve", lambda e, r=r: e.tensor_scalar(out=m2[:], in0=iota[:], scalar1=cur[:, r, 0:1], scalar2=11.0, op0=ALU.is_ge, op1=ALU.mult),
             reads=[iota, cur], writes=[m2])
        P.op("dve", lambda e, r=r: e.tensor_scalar(out=ta[:], in0=iota[:], scalar1=cur[:, r, 1:2], scalar2=1.0, op0=ALU.is_ge, op1=ALU.mult),
             reads=[iota, cur], writes=[ta])
        P.op("dve", lambda e: e.tensor_tensor(out=m2[:], in0=m2[:], in1=ta[:], op=ALU.add), reads=[m2, ta], writes=[m2])
        P.op("dve", lambda e, r=r: e.tensor_scalar(out=ta[:], in0=iota[:], scalar1=cur[:, r, 1:2], scalar2=-13.0, op0=ALU.is_gt, op1=ALU.mult),
             reads=[iota, cur], writes=[ta])
        P.op("dve", lambda e: e.tensor_tensor(out=m2[:], in0=m2[:], in1=ta[:], op=ALU.add), reads=[m2, ta], writes=[m2])
        P.op("dve", lambda e: e.memset(m2[:, 0:1], 13.0), writes=[m2])
        KTK = int(os.environ.get("KTK", "9"))
        for g in range(2 if KTK >= 2 else 0):
            pg3 = pg[g][:].rearrange("p (n f) -> p n f", f=4)
            P.op("dve", lambda e, pg3=pg3: e.tensor_reduce(out=s4[:], in_=pg3, axis=AX.X, op=ALU.add), reads=[pg[g]], writes=[s4])
            P.op("dve", lambda e, pg3=pg3: e.tensor_tensor(out=s4[:, 1:256], in0=s4[:, 1:256], in1=pg3[:, 0:255, 3], op=ALU.add),
                 reads=[s4, pg[g]], writes=[s4])
            P.op("dve", lambda e: e.tensor_tensor(out=score[:], in0=s4[:], in1=m1[:], op=ALU.mult), reads=[s4, m1], writes=[score])
            P.op("dve", lambda e: e.tensor_tensor(out=score[:], in0=score[:], in1=m2[:], op=ALU.add), reads=[score, m2], writes=[score])
            P.op("dve", lambda e: e.max(out=mx[:, 0:8], in_=score[:]), reads=[score], writes=[mx])
            P.op("dve", lambda e: e.match_replace(out=rep[:], in_to_replace=mx[:, 0:8], in_values=score[:], imm_value=-5.0),
                 reads=[score, mx], writes=[rep])
            P.op("dve", lambda e: e.max(out=mx[:, 8:16], in_=rep[:]), reads=[rep], writes=[mx])
            P.op("dve", lambda e: e.tensor_scalar(out=thr[:], in0=mx[:, 15:16], scalar1=0.0, scalar2=None, op0=ALU.max),
                 reads=[mx], writes=[thr])
            P.op("dve", lambda e: e.tensor_scalar(out=biastm[:], in0=score[:], scalar1=thr[:], scalar2=NEG, op0=ALU.is_lt, op1=ALU.mult),
                 reads=[score, thr], writes=[biastm])
            if KTK < 3:
                continue
            for j in range(2):
                P.op("pe", lambda e, j=j: e.transpose(out=ps_t[:, j * 128:(j + 1) * 128], in_=biastm[:, j * 128:(j + 1) * 128], identity=identb[:]),
                     reads=[biastm, identb], writes=[ps_t], nowaw=(j > 0))
            for t in range(2):
                P.op("dve", lambda e, g=g, t=t: e.tensor_copy(out=BT8[g][t][t * 64:(t + 1) * 64, :, :],
                                                              in_=ps_t[t * 64:(t + 1) * 64, 0:256].rearrange("p (j q) -> p j q", j=2)),
                     reads=[ps_t], writes=[BT8[g][t]])
        if KP2 < 4:
            continue
        chunks = []
        for kk in range(8 * r + 8):
            bf_ = None
            if kk >= 8 * r:
                bf_ = (identb, identb[:], dbm, dbm[:, par, kk - 8 * r, :].unsqueeze(1).to_broadcast([128, 4, 128]))
            chunks.append((kk, bf_))
        attn_branch(r, chunks, 256, 0, 1)
        if KP2 < 5:
            continue
        chunks = []
        for i in range(12):
            kk = 8 * r - 4 + i
            if kk < 0:
                continue
            chunks.append((kk, (identb, identb[:], wb, wb[:, par, i, :].unsqueeze(1).to_broadcast([128, 4, 128]))))
        attn_branch(r, chunks, 384, 132, 2)
        P.op("dve", lambda e: e.tensor_tensor(out=A1[:], in0=oa[:].rearrange("p h d -> p (h d)"), in1=szn[:], op=ALU.mult),
             reads=[oa, szn], writes=[A1])
        P.dma("pool", lab[r * 128:(r + 1) * 128, 0:512], A1[:], reads=[A1], writes=[lab], nowaw=True)
        if KP2 < 6:
            continue
        proj(ps_a, 512, wq, QR)
        P.op("act", lambda e: e.activation(out=pf2a[:, 0:512], in_=ps_a[:, 0:512], func=AF.Copy, scale=float(128 ** -0.5)),
             reads=[ps_a], writes=[pf2a])
        rotary((pf2a, pf2a[:, 0:512].rearrange("p (h d) -> p h d", h=4)), (qr, qr[:].rearrange("p (h d) -> p h d", h=4)),
               4, 128, 64, cosR[:, r, :], sinR[:, r, :])
        P.op("dve", lambda e: e.tensor_tensor(out=qd[:].rearrange("p (h d) -> p h d", h=4), in0=qr[:].rearrange("p (h d) -> p h d", h=4),
                                              in1=qdec[:].unsqueeze(2).to_broadcast([128, 4, 128]), op=ALU.mult),
             reads=[qr, qdec], writes=[qd])
        P.dma("sp", krvr2[:], lkv[r * 128:(r + 1) * 128, :], reads=[lkv], writes=[krvr2])
        for h in range(4):
            P.op("pe", lambda e, h=h: e.transpose(out=ps_t[:, h * 128:(h + 1) * 128], in_=qr[:, h * 128:(h + 1) * 128], identity=identb[:]),
                 reads=[qr, identb], writes=[ps_t], nowaw=(h > 0))
        for h in range(4):
            P.op("pe", lambda e, h=h: e.transpose(out=ps_t[:, (4 + h) * 128:(5 + h) * 128], in_=krvr2[:, h * 128:(h + 1) * 128], identity=identb[:]),
                 reads=[krvr2, identb], writes=[ps_t], nowaw=True)
        P.op("dve", lambda e: e.tensor_copy(out=qkT[:], in_=ps_t[:].rearrange("p (h t) -> p h t", h=8)),
             reads=[ps_t], writes=[qkT])
        for h in range(4):
            P.op("pe", lambda e, h=h: e.transpose(out=ps_t[:, h * 128:(h + 1) * 128], in_=qd[:, h * 128:(h + 1) * 128], identity=identb[:]),
                 reads=[qd, identb], writes=[ps_t], nowaw=(h > 0))
        P.op("dve", lambda e: e.tensor_copy(out=qdT[:], in_=ps_t[:, 0:512].rearrange("p (h t) -> p h t", h=4)),
             reads=[ps_t], writes=[qdT])
        for h in range(4):
            P.op("pe", lambda e, h=h: e.matmul(ps_b[:, h * 128:(h + 1) * 128], lhsT=qkT[:, 4 + h, :], rhs=qkT[:, h, :], start=True, stop=True),
                 reads=[qkT], writes=[ps_b], nowaw=(h > 0))
        P.op("dve", lambda e: e.tensor_tensor(out=attT[:], in0=ps_b[:, 0:512].rearrange("p (h t) -> p h t", h=4), in1=dmask[:], op=ALU.mult),
             reads=[ps_b, dmask], writes=[attT])
        for h in range(4):
            P.op("pe", lambda e, h=h: e.matmul(ps_c[:, h * 256:(h + 1) * 256], lhsT=attT[:, h, :], rhs=krvr2[:, 512 + h * 256:512 + (h + 1) * 256],
                                               start=True, stop=False), reads=[attT, krvr2], writes=[ps_c0, ps_c1], nowaw=(h > 0))
            P.op("pe", lambda e, h=h: e.matmul(ps_c[:, h * 256:(h + 1) * 256], lhsT=qdT[:, h, :], rhs=Rb[:, h * 256:(h + 1) * 256],
                                               start=False, stop=True), reads=[qdT, Rb], writes=[ps_c0, ps_c1], nowaw=True)
        P.op("act", lambda e: e.activation(out=ob[:], in_=ps_c[:], func=AF.Copy), reads=[ps_c0, ps_c1], writes=[ob])
        ob3 = ob[:].rearrange("p (h e) -> p h e", h=4)
        P.op("dve", lambda e: e.tensor_reduce(out=sm4[:], in_=ob3, axis=AX.X, op=ALU.add), reads=[ob], writes=[sm4])
        P.op("dve", lambda e: e.tensor_scalar(out=sm4[:], in0=sm4[:], scalar1=1.0 / 256, scalar2=None, op0=ALU.mult), reads=[sm4], writes=[sm4])
        P.op("dve", lambda e: e.tensor_tensor(out=ob3, in0=ob3, in1=sm4[:].unsqueeze(2).to_broadcast([128, 4, 256]), op=ALU.subtract),
             reads=[ob, sm4], writes=[ob])
        P.op("pool", lambda e: e.tensor_tensor(out=tmpkv[:], in0=ob[:], in1=ob[:], op=ALU.mult), reads=[ob], writes=[tmpkv])
        P.op("dve", lambda e: e.tensor_reduce(out=sm4[:], in_=tmpkv[:].rearrange("p (h e) -> p h e", h=4), axis=AX.X, op=ALU.add),
             reads=[tmpkv], writes=[sm4])
        P.op("dve", lambda e: e.tensor_scalar(out=sm4[:], in0=sm4[:], scalar1=1.0 / 256, scalar2=1e-6, op0=ALU.mult, op1=ALU.add),
             reads=[sm4], writes=[sm4])
        P.op("act", lambda e: e.activation(out=sm4[:], in_=sm4[:], func=AF.Ln), reads=[sm4], writes=[sm4])
        P.op("act", lambda e: e.activation(out=sm4[:], in_=sm4[:], func=AF.Exp, scale=-0.5), reads=[sm4], writes=[sm4])
        P.op("dve", lambda e: e.tensor_tensor(out=ob3, in0=ob3, in1=sm4[:].unsqueeze(2).to_broadcast([128, 4, 256]), op=ALU.mult),
             reads=[ob, sm4], writes=[ob])
        proj(ps_a, 512, wq, ZR)
        proj(ps_b, 512, wq, ZR + 512)
        P.op("act", lambda e: e.activation(out=ez[:, 0:512], in_=ps_a[:, 0:512], func=AF.Exp, scale=-1.0), reads=[ps_a], writes=[ez])
        P.op("act", lambda e: e.activation(out=ez[:, 512:1024], in_=ps_b[:, 0:512], func=AF.Exp, scale=-1.0), reads=[ps_b], writes=[ez], nowaw=True)
        P.op("dve", lambda e: e.tensor_scalar(out=ez[:], in0=ez[:], scalar1=1.0, scalar2=None, op0=ALU.add), reads=[ez], writes=[ez])
        P.op("dve", lambda e: e.reciprocal(out=ez[:], in_=ez[:]), reads=[ez], writes=[ez])
        P.op("dve", lambda e: e.tensor_tensor(out=ez[:, 0:512], in0=ps_a[:, 0:512], in1=ez[:, 0:512], op=ALU.mult), reads=[ps_a, ez], writes=[ez])
        P.op("dve", lambda e: e.tensor_tensor(out=ez[:, 512:1024], in0=ps_b[:, 0:512], in1=ez[:, 512:1024], op=ALU.mult),
             reads=[ps_b, ez], writes=[ez])
        P.op("dve", lambda e: e.tensor_tensor(out=B1[:], in0=ob[:], in1=ez[:], op=ALU.mult), reads=[ob, ez], writes=[B1])
        P.dma("pool", lab[r * 128:(r + 1) * 128, 512:1536], B1[:], reads=[B1], writes=[lab], nowaw=True)
    P.barrier()
    es3.close()
    if debug:
        P.dma("sp", dbg["lab"][:], lab[:], reads=[lab], writes=[dbg["lab"]])

    es4 = ExitStack()
    P4 = Prog.__new__(Prog)
    P4.__dict__ = P.__dict__.copy()
    P4.es = es4
    xt4 = P4.sb("xt4", [128, D], F32)
    W.update(junk=P4.sb("junk4", [128, D], BF16), ss=P4.sb("ss4", [128, 1], F32), rstd=P4.sb("rstd4", [128, 1], F32),
             xn=P4.sb("xn4", [128, D], BF16), hT=P4.sb("hT4", [128, 8, 128], BF16), rt=None)
    hT = W["hT"]
    wm = P4.sb("wm", [128, 8, 2048], BF16)
    wno = P4.sb("wno", [128, 4, 1024], BF16)
    wro = P4.sb("wro", [128, 8, 1024], BF16)
    wo = P4.sb("wo", [128, 8, 1024], BF16)
    npost = P4.sb("npost", [128, D], F32)
    stg4 = [P4.sb("wstg4_%d" % i, [128, 1024], F32) for i in range(2)]
    P.dma("sp", npost[:], npost_d.t.partition_broadcast(128), writes=[npost])
    ci = 0
    jobs = []
    for kc in range(8):
        jobs.append((w_in_d[kc * 128:(kc + 1) * 128, C_MA:C_MA + 1024], wm, (kc, 0)))
        jobs.append((w_in_d[kc * 128:(kc + 1) * 128, C_MB:C_MB + 1024], wm, (kc, 1024)))
        jobs.append((wro_d[kc * 128:(kc + 1) * 128, :], wro, (kc, 0)))
        jobs.append((wo_d[kc * 128:(kc + 1) * 128, :], wo, (kc, 0)))
        if kc < 4:
            jobs.append((wno_d[kc * 128:(kc + 1) * 128, :], wno, (kc, 0)))
    for src, dstb, (kc, o0) in jobs:
        st = stg4[ci % 2]
        P.dma("sp", st[:], src, writes=[st])
        en = cast_eng[ci % 3]
        if en == "act":
            P.op("act", lambda e, st=st, dstb=dstb, kc=kc, o0=o0: e.activation(out=dstb[:, kc, o0:o0 + 1024], in_=st[:], func=AF.Copy),
                 reads=[st], writes=[dstb], nowaw=True)
        else:
            P.op(en, lambda e, st=st, dstb=dstb, kc=kc, o0=o0: e.tensor_copy(out=dstb[:, kc, o0:o0 + 1024], in_=st[:]),
                 reads=[st], writes=[dstb], nowaw=True)
        ci += 1
    AB = P4.sb("AB", [128, 1536], BF16)
    ABT = P4.sb("ABT", [128, 12, 128], BF16)
    sg = P4.sb("sg", [128, 512], F32)
    mg = P4.sb("mg", [128, 1024], F32)
    tq = P4.sb("tq", [128, 512], F32)
    mgb = P4.sb("mgb", [128, 1024], BF16)
    mT = P4.sb("mT", [128, 8, 128], BF16)
    yo = P4.sb("yo", [128, D], F32)
    ss5 = P4.sb("ss5", [128, 1], F32)
    junk5 = P4.sb("junk5", [128, D], BF16)
    KP3 = int(os.environ.get("KP3", "99"))
    for r in range(NR if KP3 >= 1 else 0):
        P.dma("sp", xt4[:], x_d[r], writes=[xt4])
        rms_front(xt4, gcol, hT)
        P.dma("sp", AB[:], lab[r * 128:(r + 1) * 128, :], reads=[lab], writes=[AB])
        for j in range(8):
            P.op("pe", lambda e, j=j: e.transpose(out=ps_t[:, j * 128:(j + 1) * 128], in_=AB[:, j * 128:(j + 1) * 128], identity=identb[:]),
                 reads=[AB, identb], writes=[ps_t], nowaw=(j > 0))
        P.op("dve", lambda e: e.tensor_copy(out=ABT[:, 0:8, :], in_=ps_t[:].rearrange("p (h t) -> p h t", h=8)),
             reads=[ps_t], writes=[ABT])
        for j in range(4):
            P.op("pe", lambda e, j=j: e.transpose(out=ps_t[:, j * 128:(j + 1) * 128], in_=AB[:, (8 + j) * 128:(9 + j) * 128], identity=identb[:]),
                 reads=[AB, identb], writes=[ps_t], nowaw=(j > 0))
        P.op("dve", lambda e: e.tensor_copy(out=ABT[:, 8:12, :], in_=ps_t[:, 0:512].rearrange("p (h t) -> p h t", h=4)),
             reads=[ps_t], writes=[ABT], nowaw=True)
        for half in range(2):
            hs = slice(half * 512, (half + 1) * 512)
            for j in range(4):
                P.op("pe", lambda e, j=j, hs=hs: e.matmul(ps_a[:, 0:512], lhsT=ABT[:, j, :], rhs=wno[:, j, hs], start=(j == 0), stop=(j == 3)),
                     reads=[ABT, wno], writes=[ps_a], nowaw=(j > 0))
            proj(ps_b, 512, wm, half * 512)
            P.op("act", lambda e: e.activation(out=sg[:], in_=ps_b[:, 0:512], func=AF.Exp, scale=-1.0), reads=[ps_b], writes=[sg])
            P.op("dve", lambda e: e.tensor_scalar(out=sg[:], in0=sg[:], scalar1=1.0, scalar2=None, op0=ALU.add), reads=[sg], writes=[sg])
            P.op("dve", lambda e: e.reciprocal(out=sg[:], in_=sg[:]), reads=[sg], writes=[sg])
            P.op("dve", lambda e, hs=hs: e.tensor_tensor(out=mg[:, hs], in0=ps_a[:, 0:512], in1=sg[:], op=ALU.mult),
                 reads=[ps_a, sg], writes=[mg], nowaw=(half > 0))
            for j in range(8):
                P.op("pe", lambda e, j=j, hs=hs: e.matmul(ps_d[:, 0:512], lhsT=ABT[:, 4 + j, :], rhs=wro[:, j, hs], start=(j == 0), stop=(j == 7)),
                     reads=[ABT, wro], writes=[ps_d], nowaw=(j > 0))
            proj(ps_e, 512, wm, 1024 + half * 512)
            P.op("act", lambda e: e.activation(out=sg[:], in_=ps_e[:, 0:512], func=AF.Exp, scale=-1.0), reads=[ps_e], writes=[sg])
            P.op("dve", lambda e: e.tensor_scalar(out=sg[:], in0=sg[:], scalar1=1.0, scalar2=None, op0=ALU.add), reads=[sg], writes=[sg])
            P.op("dve", lambda e: e.reciprocal(out=sg[:], in_=sg[:]), reads=[sg], writes=[sg])
            P.op("dve", lambda e: e.tensor_tensor(out=tq[:], in0=ps_d[:, 0:512], in1=sg[:], op=ALU.mult), reads=[ps_d, sg], writes=[tq])
            P.op("dve", lambda e, hs=hs: e.tensor_tensor(out=mg[:, hs], in0=mg[:, hs], in1=tq[:], op=ALU.add), reads=[mg, tq], writes=[mg])
        P.op("dve", lambda e: e.tensor_copy(out=mgb[:], in_=mg[:]), reads=[mg], writes=[mgb])
        for j in range(8):
            P.op("pe", lambda e, j=j: e.transpose(out=ps_t[:, j * 128:(j + 1) * 128], in_=mgb[:, j * 128:(j + 1) * 128], identity=identb[:]),
                 reads=[mgb, identb], writes=[ps_t], nowaw=(j > 0))
        P.op("dve", lambda e: e.tensor_copy(out=mT[:], in_=ps_t[:].rearrange("p (h t) -> p h t", h=8)), reads=[ps_t], writes=[mT])
        for half in range(2):
            hs = slice(half * 512, (half + 1) * 512)
            for k in range(8):
                P.op("pe", lambda e, k=k, hs=hs: e.matmul(ps_c[:, hs], lhsT=mT[:, k, :], rhs=wo[:, k, hs], start=(k == 0), stop=(k == 7)),
                     reads=[mT, wo], writes=[ps_c0, ps_c1], nowaw=not (half == 0 and k == 0))
        P.op("act", lambda e: e.activation(out=junk5[:], in_=ps_c[:], func=AF.Square, accum_out=ss5[:]), reads=[ps_c0, ps_c1], writes=[junk5, ss5])
        P.op("dve", lambda e: e.tensor_scalar(out=ss5[:], in0=ss5[:], scalar1=1.0 / D, scalar2=1e-6, op0=ALU.mult, op1=ALU.add),
             reads=[ss5], writes=[ss5])
        P.op("act", lambda e: e.activation(out=ss5[:], in_=ss5[:], func=AF.Ln), reads=[ss5], writes=[ss5])
        P.op("act", lambda e: e.activation(out=ss5[:], in_=ss5[:], func=AF.Exp, scale=-0.5), reads=[ss5], writes=[ss5])
        P.op("dve", lambda e: e.scalar_tensor_tensor(out=yo[:], in0=ps_c[:], scalar=ss5[:], in1=npost[:], op0=ALU.mult, op1=ALU.mult),
             reads=[ps_c0, ps_c1, ss5, npost], writes=[yo])
        P.op("dve", lambda e: e.tensor_tensor(out=yo[:], in0=yo[:], in1=xt4[:], op=ALU.add), reads=[yo, xt4], writes=[yo])
        P.dma("pool", y_d[r], yo[:], reads=[yo], writes=[y_d], nowaw=True)
    P.barrier()
    es4.close()

    P.emit()
    es.close()
    return nc


_CACHE = {}


def _consts():
    identb = np.eye(128, dtype=np.float32).astype(ml_dtypes.bfloat16)
    identf = np.eye(128, dtype=np.float32)
    invn = (1.0 / (np.float32(500000.0) ** (np.arange(0, 16, 2, dtype=np.float32) / np.float32(16)))).astype(np.float32)
    invr = (1.0 / (np.float32(10000.0) ** np.linspace(0.0, 1.0, 64, dtype=np.float32))).astype(np.float32)
    lg = np.log(1.0 - 2.0 ** (-5.0 - np.arange(4, dtype=np.float64)))
    n = np.arange(128, dtype=np.float64)
    kdec = np.exp((127.0 - n)[:, None] * lg[None, :]).astype(np.float32)
    gam = 1.0 - 2.0 ** (-5.0 - np.arange(4, dtype=np.float64))
    qdec = (gam[None, :] ** (n[:, None] + 1.0)).astype(np.float32)
    dif = n[None, :] - n[:, None]
    dmask = np.where(dif[:, None, :] >= 0, gam[None, :, None] ** np.maximum(dif[:, None, :], 0.0), 0.0).astype(np.float32)
    e32 = np.zeros((128, 32, 128), dtype=np.float32)
    for kkl in range(32):
        for key in range(128):
            e32[2 * kkl + key // 64, kkl, key] = 1.0
            e32[64 + 2 * kkl + key // 64, kkl, key] = 1.0
    iota = np.broadcast_to(np.arange(256, dtype=np.float32)[None, :], (128, 256)).copy()
    return dict(identb=identb, identf=identf, kdec=kdec, qdec=qdec, dmask=dmask,
                e32=e32.astype(ml_dtypes.bfloat16), iota=iota), invn, invr


def _core_tables(c):
    gam = 1.0 - 2.0 ** (-5.0 - np.arange(4, dtype=np.float64))
    key = np.arange(128)[:, None]
    q = np.arange(128)[None, :]
    wb = np.zeros((128, 2, 12, 128), dtype=np.float32)
    db = np.zeros((128, 2, 8, 128), dtype=np.float32)
    cb = np.zeros((128, 2, 72), dtype=np.float32)
    for par in range(2):
        j = c if par == 0 else 7 - c
        for i in range(12):
            diff = 128 * (j + 4 - i) + q - key
            wb[:, par, i, :] = np.where((diff >= 0) & (diff < 512), 0.0, NEG)
        for i in range(8):
            if i < j:
                db[:, par, i, :] = 0.0
            elif i == j:
                db[:, par, i, :] = np.where(key <= q, 0.0, NEG)
            else:
                db[:, par, i, :] = NEG
        ii = np.arange(72)[None, :]
        qq = np.arange(128)[:, None]
        cb[:, par, :] = np.where(16 * ii - 97 <= 128 * j + qq, 0.0, NEG)
    tiles = [tile_of(c, r) for r in range(NR)]
    cur = np.zeros((128, NR, 2), dtype=np.float32)
    for r in range(NR):
        cv = 2 * tiles[r] + (np.arange(128) >= 64)
        cur[:, r, 0] = cv - 1
        cur[:, r, 1] = cv
    wret = np.zeros((NR, 16, 4), dtype=np.float64)
    dfac = np.ones((NR, 4), dtype=np.float64)
    for r in range(NR):
        b = tiles[r]
        bp = tiles[r - 1] if r > 0 else 0
        if r > 0:
            dfac[r] = gam ** (128.0 * (b - bp))
        for mi in range(16):
            m = 8 * (r - 1) + mi
            if m >= bp and m < b and m >= 0:
                wret[r, mi] = gam ** (128.0 * (b - 1 - m))
    return dict(wb=wb.astype(ml_dtypes.bfloat16), db=db.astype(ml_dtypes.bfloat16), cb=cb, cur=cur,
                wret=np.broadcast_to(wret.astype(np.float32)[None], (128, NR, 16, 4)).copy(),
                dfac=np.broadcast_to(dfac.astype(np.float32)[None], (128, NR, 4)).copy())


def kernel(x, norm_pre, w_in, b_nsa_gate, cmp_pe_k, cmp_w1_k, cmp_w2_k,
           cmp_pe_v, cmp_w1_v, cmp_w2_v, w_nsa_o, w_ret_o, w_out, norm_post, _debug=False):
    x = np.asarray(x, dtype=np.float32).reshape(S, D)
    xt = x.reshape(128, 128, D)
    key = "dbg" if _debug else "nc"
    if key not in _CACHE:
        _CACHE[key] = build(debug=_debug)
    nc = _CACHE[key]
    cs, invn, invr = _consts()
    slot_tiles = [tile_of(sl // NR, sl % NR) for sl in range(128)]
    x_all = np.ascontiguousarray(xt[slot_tiles])
    pos_all = (np.array(slot_tiles, dtype=np.float32)[None, :] * 128 + np.arange(128, dtype=np.float32)[:, None])
    angn_a = (pos_all[:, :, None] * invn[None, None, :]).astype(np.float32)
    angr_a = (pos_all[:, :, None] * invr[None, None, :]).astype(np.float32)
    cs["x_all"] = x_all
    cs["cosn_all"] = np.cos(angn_a).astype(np.float32)
    cs["sinn_all"] = np.sin(angn_a).astype(np.float32)
    cs["cosr_all"] = np.ascontiguousarray(np.cos(angr_a).astype(np.float32).transpose(1, 0, 2))
    cs["sinr_all"] = np.ascontiguousarray(np.sin(angr_a).astype(np.float32).transpose(1, 0, 2))
    in_maps = []
    for c in range(NCORES):
        tiles = [tile_of(c, r) for r in range(NR)]
        m = dict(cs)
        m["x"] = np.ascontiguousarray(xt[tiles])
        posc = (np.array(tiles, dtype=np.float32)[None, :] * 128 + np.arange(128, dtype=np.float32)[:, None])
        angn = (posc[:, :, None] * invn[None, None, :]).astype(np.float32)
        angr = (posc[:, :, None] * invr[None, None, :]).astype(np.float32)
        m["cosn"] = np.cos(angn).astype(np.float32)
        m["sinn"] = np.sin(angn).astype(np.float32)
        m["cosr"] = np.cos(angr).astype(np.float32)
        m["sinr"] = np.sin(angr).astype(np.float32)
        m["w_in"] = np.asarray(w_in, dtype=np.float32)
        m["norm_pre"] = np.asarray(norm_pre, dtype=np.float32)
        m.update(_core_tables(c))
        for nm, arr in (("cmp_w1_k", cmp_w1_k), ("cmp_w2_k", cmp_w2_k), ("cmp_pe_k", cmp_pe_k),
                        ("cmp_w1_v", cmp_w1_v), ("cmp_w2_v", cmp_w2_v), ("cmp_pe_v", cmp_pe_v),
                        ("b_nsa_gate", b_nsa_gate), ("norm_post", norm_post), ("w_nsa_o", w_nsa_o),
                        ("w_ret_o", w_ret_o), ("w_out", w_out)):
            m[nm] = np.asarray(arr, dtype=np.float32)
        in_maps.append(m)
    res = run_bass_kernel_spmd(nc, in_maps, core_ids=list(range(NCORES)))
    out = np.zeros((128, 128, D), dtype=np.float32)
    for c in range(NCORES):
        y = res.results[c]["y"]
        for r in range(NR):
            out[tile_of(c, r)] = y[r]
    if _debug:
        return out.reshape(1, S, D), res
    return out.reshape(1, S, D)
```
